# Optimizing a Trainium2 kernel written in Bass

```python
import jax, jax.numpy as jnp
from jax import lax
import numpy as np

D_MODEL = 1024
BATCH = 8
SEQ = 4096
DEPTH = 2

HEAD_DIM = 64
ATTN_Q_HEADS = 8
ATTN_KV_HEADS = 2
ATTN_GROUP = ATTN_Q_HEADS // ATTN_KV_HEADS
WINDOW = 128
ATTN_BLOCK = 128
ROPE_THETA = 10000.0
ATTN_WIDTH = ATTN_Q_HEADS * HEAD_DIM
KV_WIDTH = ATTN_KV_HEADS * HEAD_DIM
CONV_WIDTH = D_MODEL // 2
SHORT_CONV_K = 3
HY_SPLITS = (ATTN_WIDTH, KV_WIDTH, KV_WIDTH, CONV_WIDTH, CONV_WIDTH, CONV_WIDTH)
HY_IN_WIDTH = sum(HY_SPLITS)
HY_OUT_WIDTH = ATTN_WIDTH + CONV_WIDTH
GLA_HEADS = 4
GLA_KEY_DIM = D_MODEL // 2
GLA_VAL_DIM = D_MODEL
GLA_DK = GLA_KEY_DIM // GLA_HEADS
GLA_DV = GLA_VAL_DIM // GLA_HEADS
GLA_GATE_RANK = 16
GLA_GATE_NORMALIZER = 16.0
GLA_CHUNK = 64
GLA_SPLITS = (GLA_KEY_DIM, GLA_KEY_DIM, GLA_VAL_DIM, GLA_GATE_RANK, GLA_VAL_DIM)
GLA_IN_WIDTH = sum(GLA_SPLITS)
D_FF = 2816
FFN_CONV_K = 3
PLE_DIM = 256
MAX_POS_OFFSET = 1024
N_EVEN = (DEPTH + 1) // 2
N_ODD = DEPTH // 2
RMS_EPS = 1e-6

kernel_name = 'hybrid_swa_shortconv_gla_convffn'


def _split(z, widths):
    idx, acc = [], 0
    for w in widths[:-1]:
        acc += w
        idx.append(acc)
    return jnp.split(z, idx, axis=-1)


def _rms_norm(x, gain):
    xf = x.astype(jnp.float32)
    y = xf * lax.rsqrt(jnp.mean(xf * xf, axis=-1, keepdims=True) + RMS_EPS)
    return (y * gain.astype(jnp.float32)).astype(x.dtype)


def _causal_depthwise_conv(x, w):
    k = w.shape[0]
    return lax.conv_general_dilated(
        x, w[:, None, :].astype(x.dtype), window_strides=(1,), padding=[(k - 1, 0)],
        dimension_numbers=('NWC', 'WIO', 'NWC'), feature_group_count=x.shape[-1])


def _rope_tables(positions):
    inv_freq = ROPE_THETA ** (-jnp.arange(0, HEAD_DIM, 2, dtype=jnp.float32) / HEAD_DIM)
    ang = positions.astype(jnp.float32)[..., None] * inv_freq
    return jnp.cos(ang)[:, :, None, :], jnp.sin(ang)[:, :, None, :]


def _apply_rope(x, cos, sin):
    x1, x2 = jnp.split(x.astype(jnp.float32), 2, axis=-1)
    return jnp.concatenate([x1 * cos - x2 * sin, x2 * cos + x1 * sin], axis=-1).astype(x.dtype)


def _sliding_window_sink_attention(q, k, v, sinks):
    b, s = q.shape[0], q.shape[1]
    nb = s // ATTN_BLOCK
    qb = q.reshape(b, nb, ATTN_BLOCK, ATTN_KV_HEADS, ATTN_GROUP, HEAD_DIM)

    def band(t):
        tb = t.reshape(b, nb, ATTN_BLOCK, ATTN_KV_HEADS, HEAD_DIM)
        prev = jnp.pad(tb[:, :-1], ((0, 0), (1, 0), (0, 0), (0, 0), (0, 0)))
        return jnp.concatenate([prev, tb], axis=2)

    kband, vband = band(k), band(v)
    scores = jnp.einsum('bnqhgd,bnkhd->bnhgqk', qb, kband).astype(jnp.float32) * (HEAD_DIM ** -0.5)
    blk = jnp.arange(nb)[:, None] * ATTN_BLOCK
    q_pos = blk + jnp.arange(ATTN_BLOCK)[None, :]
    k_pos = blk - ATTN_BLOCK + jnp.arange(2 * ATTN_BLOCK)[None, :]
    dist = q_pos[:, :, None] - k_pos[:, None, :]
    allowed = (dist >= 0) & (dist < WINDOW) & (k_pos[:, None, :] >= 0)
    scores = jnp.where(allowed[None, :, None, None], scores, -jnp.inf)
    sink = jnp.broadcast_to(
        sinks.astype(jnp.float32).reshape(1, 1, ATTN_KV_HEADS, ATTN_GROUP, 1, 1), scores.shape[:-1] + (1,))
    probs = jax.nn.softmax(jnp.concatenate([scores, sink], axis=-1), axis=-1)[..., :-1]
    out = jnp.einsum('bnhgqk,bnkhd->bnqhgd', probs.astype(v.dtype), vband)
    return out.reshape(b, s, ATTN_WIDTH)


def _attn_conv_mixer(h, cos, sin, w_in, q_norm, k_norm, sinks, conv_w, w_out):
    b, s, _ = h.shape
    q, k, v, cb, cc, cx = _split(h @ w_in, HY_SPLITS)
    q = _apply_rope(_rms_norm(q.reshape(b, s, ATTN_Q_HEADS, HEAD_DIM), q_norm), cos, sin)
    k = _apply_rope(_rms_norm(k.reshape(b, s, ATTN_KV_HEADS, HEAD_DIM), k_norm), cos, sin)
    v = v.reshape(b, s, ATTN_KV_HEADS, HEAD_DIM)
    attn = _sliding_window_sink_attention(q, k, v, sinks)
    conv = cb * _causal_depthwise_conv(cc * cx, conv_w)
    return jnp.concatenate([attn, conv], axis=-1) @ w_out


def _gla_chunked(q, k, v, log_a):
    b, s, nh, dk = q.shape
    dv = v.shape[-1]
    nc = s // GLA_CHUNK

    def chunks(t):
        return t.astype(jnp.float32).reshape(b, nc, GLA_CHUNK, nh, t.shape[-1]).transpose(1, 0, 3, 2, 4)

    qc, kc, vc = chunks(q), chunks(k), chunks(v)
    gc = jnp.cumsum(chunks(log_a), axis=3)
    causal = jnp.tril(jnp.ones((GLA_CHUNK, GLA_CHUNK), dtype=bool))

    def step(state, inp):
        qi, ki, vi, gi = inp
        g_end = gi[:, :, -1, :]
        o_inter = jnp.einsum('bhtd,bhdv->bhtv', qi * jnp.exp(gi), state)
        rel = gi[:, :, :, None, :] - gi[:, :, None, :, :]
        decay = jnp.exp(jnp.where(causal[:, :, None], rel, -jnp.inf))
        scores = jnp.einsum('bhtd,bhsd,bhtsd->bhts', qi, ki, decay)
        o_intra = jnp.einsum('bhts,bhsv->bhtv', scores, vi)
        state = jnp.exp(g_end)[..., None] * state + jnp.einsum(
            'bhsd,bhsv->bhdv', ki * jnp.exp(g_end[:, :, None, :] - gi), vi)
        return state, o_inter + o_intra

    state0 = jnp.zeros((b, nh, dk, dv), jnp.float32)
    _, o = lax.scan(step, state0, (qc, kc, vc, gc))
    return o.transpose(1, 0, 3, 2, 4).reshape(b, s, nh, dv)


def _gla_mixer(h, w_in, w_gate_up, gate_bias, o_norm, w_out):
    b, s, _ = h.shape
    q, k, v, g_lr, og = _split(h @ w_in, GLA_SPLITS)
    log_a = jax.nn.log_sigmoid((g_lr @ w_gate_up + gate_bias).astype(jnp.float32)) / GLA_GATE_NORMALIZER

    def heads(t, d):
        return t.reshape(b, s, GLA_HEADS, d)

    o = _gla_chunked(heads(q, GLA_DK) * (GLA_DK ** -0.5), heads(k, GLA_DK), heads(v, GLA_DV), heads(log_a, GLA_DK))
    o = _rms_norm(o.astype(h.dtype), o_norm) * jax.nn.silu(heads(og, GLA_DV))
    return o.reshape(b, s, GLA_VAL_DIM) @ w_out


def _conv_ffn(h, w_up, conv_w, conv_b, w_down):
    u = _causal_depthwise_conv(h @ w_up, conv_w) + conv_b
    gate, up = jnp.split(u, 2, axis=-1)
    return (jax.nn.gelu(gate, approximate=False) * up) @ w_down


def _per_layer_embedding(x, p_i, norm, w_gate, w_proj):
    gate = jax.nn.sigmoid(_rms_norm(x, norm) @ w_gate)
    return gate * (p_i @ w_proj)


def setup_inputs(seed: int = 0) -> dict:
    key = jax.random.key(seed)
    ks = jax.random.split(key, 24)
    f32 = jnp.float32

    def normal(k, shape, fan_in):
        return jax.random.normal(k, shape, f32) * (fan_in ** -0.5)

    def gain(k, shape):
        return 1.0 + 0.05 * jax.random.normal(k, shape, f32)

    x = jax.random.normal(ks[0], (BATCH, SEQ, D_MODEL), f32)
    p = jax.random.normal(ks[1], (DEPTH, BATCH, SEQ, PLE_DIM), f32)
    offset = jax.random.randint(ks[2], (BATCH, 1), 0, MAX_POS_OFFSET, dtype=jnp.int32)
    positions = offset + jnp.arange(SEQ, dtype=jnp.int32)[None, :]
    return {
        'x': x,
        'p': p,
        'positions': positions,
        'mix_norm': gain(ks[3], (DEPTH, D_MODEL)),
        'ffn_norm': gain(ks[4], (DEPTH, D_MODEL)),
        'ffn_w_up': normal(ks[5], (DEPTH, D_MODEL, 2 * D_FF), D_MODEL),
        'ffn_conv_w': normal(ks[6], (DEPTH, FFN_CONV_K, 2 * D_FF), FFN_CONV_K),
        'ffn_conv_b': 0.02 * jax.random.normal(ks[7], (DEPTH, 2 * D_FF), f32),
        'ffn_w_down': normal(ks[8], (DEPTH, D_FF, D_MODEL), D_FF),
        'ple_norm': gain(ks[9], (DEPTH, D_MODEL)),
        'ple_w_gate': normal(ks[10], (DEPTH, D_MODEL, D_MODEL), D_MODEL),
        'ple_w_proj': normal(ks[11], (DEPTH, PLE_DIM, D_MODEL), PLE_DIM),
        'hy_w_in': normal(ks[12], (N_EVEN, D_MODEL, HY_IN_WIDTH), D_MODEL),
        'hy_q_norm': gain(ks[13], (N_EVEN, HEAD_DIM)),
        'hy_k_norm': gain(ks[14], (N_EVEN, HEAD_DIM)),
        'hy_sinks': jax.random.normal(ks[15], (N_EVEN, ATTN_Q_HEADS), f32),
        'hy_conv_w': normal(ks[16], (N_EVEN, SHORT_CONV_K, CONV_WIDTH), SHORT_CONV_K),
        'hy_w_out': normal(ks[17], (N_EVEN, HY_OUT_WIDTH, D_MODEL), HY_OUT_WIDTH),
        'gla_w_in': normal(ks[18], (N_ODD, D_MODEL, GLA_IN_WIDTH), D_MODEL),
        'gla_w_gate_up': normal(ks[19], (N_ODD, GLA_GATE_RANK, GLA_KEY_DIM), GLA_GATE_RANK),
        'gla_gate_bias': 0.1 * jax.random.normal(ks[20], (N_ODD, GLA_KEY_DIM), f32),
        'gla_o_norm': gain(ks[21], (N_ODD, GLA_DV)),
        'gla_w_out': normal(ks[22], (N_ODD, GLA_VAL_DIM, D_MODEL), GLA_VAL_DIM),
    }


def reference(x, p, positions, mix_norm, ffn_norm, ffn_w_up, ffn_conv_w, ffn_conv_b, ffn_w_down,
              ple_norm, ple_w_gate, ple_w_proj, hy_w_in, hy_q_norm, hy_k_norm, hy_sinks, hy_conv_w,
              hy_w_out, gla_w_in, gla_w_gate_up, gla_gate_bias, gla_o_norm, gla_w_out):
    cos, sin = _rope_tables(positions)
    for i in range(DEPTH):
        j = i // 2
        h = _rms_norm(x, mix_norm[i])
        if i % 2 == 0:
            x = x + _attn_conv_mixer(h, cos, sin, hy_w_in[j], hy_q_norm[j], hy_k_norm[j], hy_sinks[j],
                                     hy_conv_w[j], hy_w_out[j])
        else:
            x = x + _gla_mixer(h, gla_w_in[j], gla_w_gate_up[j], gla_gate_bias[j], gla_o_norm[j], gla_w_out[j])
        x = x + _conv_ffn(_rms_norm(x, ffn_norm[i]), ffn_w_up[i], ffn_conv_w[i], ffn_conv_b[i], ffn_w_down[i])
        x = x + _per_layer_embedding(x, p[i], ple_norm[i], ple_w_gate[i], ple_w_proj[i])
    return x
```

```python
import numpy as np
from contextlib import ExitStack
import concourse.bass as bass
import concourse.mybir as mybir
from concourse.bass_utils import run_bass_kernel_spmd

F32 = mybir.dt.float32
BF16 = mybir.dt.bfloat16
I32 = mybir.dt.int32
AF = mybir.ActivationFunctionType
ALU = mybir.AluOpType

ENGS = ("pe", "act", "dve", "pool", "sp")
N_DMA_SEMS = 24


class _Op:
    __slots__ = ("eng", "fn", "reads", "writes", "dma", "deps", "signal", "milestone",
                 "dma_sem", "dma_val", "dma_prev", "idx", "fence", "tag")


class Prog:
    def __init__(self):
        self.ops = []
        self.tag = ""
        self.annotate = False

    def op(self, eng, fn, reads=(), writes=(), dma=False):
        o = _Op()
        o.eng = eng
        o.fn = fn
        o.reads = tuple(reads)
        o.writes = tuple(writes)
        o.dma = dma
        o.deps = None
        o.signal = False
        o.milestone = 0
        o.dma_sem = -1
        o.dma_val = 0
        o.dma_prev = 0
        o.idx = len(self.ops)
        o.fence = False
        o.tag = self.tag
        self.ops.append(o)
        return o

    def pe(self, fn, reads=(), writes=()):
        return self.op("pe", fn, reads, writes)

    def act(self, fn, reads=(), writes=()):
        return self.op("act", fn, reads, writes)

    def dve(self, fn, reads=(), writes=()):
        return self.op("dve", fn, reads, writes)

    def pool(self, fn, reads=(), writes=()):
        return self.op("pool", fn, reads, writes)

    def dma(self, eng, fn, reads=(), writes=()):
        return self.op(eng, fn, reads, writes, dma=True)

    def analyze(self):
        last_writer = {}
        readers = {}
        ops = self.ops
        for o in ops:
            deps = set()
            for r in o.reads:
                w = last_writer.get(r)
                if w is not None:
                    deps.add(w)
                if o.fence:
                    for rd in readers.get(r, ()):
                        deps.add(rd)
            for k in o.writes:
                w = last_writer.get(k)
                if w is not None:
                    deps.add(w)
                for rd in readers.get(k, ()):
                    deps.add(rd)
            deps.discard(o.idx)
            dl = []
            for d in sorted(deps):
                od = ops[d]
                if (not od.dma) and (not o.dma) and od.eng == "pe" and o.eng == "pe":
                    continue
                dl.append(d)
                if not od.dma:
                    assert od.fn is not None
                    od.signal = True
            o.deps = dl
            if o.fence:
                continue
            for r in o.reads:
                readers.setdefault(r, []).append(o.idx)
            for k in o.writes:
                last_writer[k] = o.idx
                readers[k] = []
        cnt = {e: 0 for e in ENGS}
        for o in ops:
            if o.dma:
                continue
            if o.signal:
                cnt[o.eng] += 1
                o.milestone = cnt[o.eng]
        pools = {"sp": list(range(0, 16)), "pool": list(range(16, 20)), "act": list(range(20, N_DMA_SEMS))}
        use = [0] * N_DMA_SEMS
        cnt_q = {}
        k = 0
        for o in ops:
            if o.dma:
                pl = pools[o.eng]
                s = pl[cnt_q.get(o.eng, 0) % len(pl)]
                cnt_q[o.eng] = cnt_q.get(o.eng, 0) + 1
                k += 1
                o.dma_sem = s
                o.dma_prev = use[s] * 16
                use[s] += 1
                o.dma_val = use[s] * 16
        self.n_dma = k

    def emit(self, nc, stack):
        self.analyze()
        ops = self.ops
        esem = {e: stack.enter_context(nc.semaphore("s_" + e)) for e in ENGS}
        dsem = [stack.enter_context(nc.semaphore("d%d" % i)) for i in range(N_DMA_SEMS)]
        per_eng = {e: [o for o in ops if o.eng == e] for e in ENGS}
        block = stack.enter_context(nc.Block())

        def run(engname, eng):
            waited = {}

            def wait(key, sem, val):
                if val <= 0:
                    return
                if waited.get(key, 0) >= val:
                    return
                waited[key] = val
                eng.wait_ge(sem, val)

            for o in per_eng[engname]:
                for d in o.deps:
                    od = ops[d]
                    if od.dma:
                        wait(("d", od.dma_sem), dsem[od.dma_sem], od.dma_val)
                    else:
                        wait(("e", od.eng), esem[od.eng], od.milestone)
                if o.dma:
                    wait(("d", o.dma_sem), dsem[o.dma_sem], o.dma_prev)
                if o.fn is None:
                    continue
                ins = o.fn(eng)
                if self.annotate and o.tag:
                    ins.annotate(o.tag)
                if o.dma:
                    ins.then_inc(dsem[o.dma_sem], 16)
                elif o.signal:
                    ins.then_inc(esem[engname], 1)

        @block.tensor
        def _(e):
            run("pe", e)

        @block.scalar
        def _(e):
            run("act", e)

        @block.vector
        def _(e):
            run("dve", e)

        @block.gpsimd
        def _(e):
            run("pool", e)

        @block.sync
        def _(e):
            run("sp", e)


class V:
    __slots__ = ("ap", "keys")

    def __init__(self, ap, keys):
        self.ap = ap
        self.keys = tuple(keys)


def _k(x):
    return x.keys if isinstance(x, V) else ()


def _a(x):
    return x.ap if isinstance(x, V) else x


D_MODEL = 1024
TT = 512
NQ = 4
D_FF = 2816
NFF = 22
RMS_EPS = 1e-6
TWO_PI = 6.283185307179586
CW_C1 = 6.28125
CW_C2 = TWO_PI - 6.28125

C_MIXN, C_FFNN, C_PLEN = 0, 16, 32
C_CW = 48
C_CB = 312
C_HYCW = 400
C_QN, C_KN, C_INVF, C_SGN = 412, 413, 414, 415
C_SINK = 416
C_ONORM = 420
NCOLS = 422
B_ONESK, B_BLK64, B_PSWAP, B_ONESPAD, B_ONES256, B_MOWN, B_MPREV, B_IDENT = 0, 128, 256, 384, 640, 768, 1280, 1792
NCB = 1920


def build(S=4096, stop_after=None):
    NT = S // TT
    nc = bass.Bass("TRN2", target_bir_lowering=False)

    def din(name, shape, dt=F32):
        return nc.dram_tensor(name, shape, dt, kind="ExternalInput").ap()

    x_d = din("x", [S, 1024])
    p_d = din("p", [2, S, 256])
    pos_d = din("pos", [1, S], I32)
    w_hy_in = din("hy_w_in", [1024, 2304])
    w_hy_out = din("hy_w_out", [1024, 1024])
    w_up = [din("ffn_w_up%d" % l, [1024, 5632]) for l in range(2)]
    w_down = [din("ffn_w_down%d" % l, [2816, 1024]) for l in range(2)]
    w_pg = [din("ple_w_gate%d" % l, [1024, 1024]) for l in range(2)]
    w_pp = [din("ple_w_proj%d" % l, [256, 1024]) for l in range(2)]
    w_gla_in = din("gla_w_in", [1024, 3088])
    w_gla_out = din("gla_w_out", [1024, 1024])
    wgu_d = din("gla_wgu_aug", [17, 512])
    cols_d = din("cols", [128, NCOLS])
    ident_d = din("ident", [128, 128])
    tri_d = din("triu", [128, 128])
    cbf_d = din("cbf", [128, NCB])
    y_d = nc.dram_tensor("y", [S, 1024], F32, kind="ExternalOutput").ap()

    P = Prog()
    st = ExitStack()
    with st:
        def sb(name, shape, dt):
            return st.enter_context(nc.sbuf_tensor("sb_" + name, shape, dt))

        xT_t = sb("xT", [128, 8, TT], F32)
        hT_t = sb("hT", [128, 8, TT], BF16)
        xin_t = sb("xin", [128, 2, 1024], F32)
        xout_t = sb("xout", [128, 2, 1024], F32)
        pin_ap = xout_t[:, 0, :].rearrange("p (b d) -> p b d", b=NQ)
        PIN_KEY = ("xout", 0)
        cs_t = sb("cs", [128, 2, TT], F32)
        sqt_t = sb("sqt", [128, 3, TT], BF16)
        posi_t = sb("posi", [128, TT], I32)
        pT_t = sb("pT", [128, 2, TT], BF16)
        NSLOT, SLOT = 6, 4096
        ring_t = sb("ring", [128, NSLOT, SLOT], BF16)
        actp_t = sb("actp", [128, NFF * TT // 2], F32)
        NTMP = 10
        tmp_t = sb("tmp", [128, NTMP, TT], F32)
        cols_t = sb("cols", [128, NCOLS], F32)
        ident_t = sb("ident", [128, 128], F32)
        tri_t = sb("tri", [128, 128], F32)
        cbf_t = sb("cbf", [128, NCB], BF16)
        tails_t = sb("tails", [128, 2, 44, 2], F32)
        fix_t = sb("fix", [128, 2, 44, 2], F32)
        fixtmp_t = sb("fixtmp", [128, 44], F32)
        sinke_t = sb("sinke", [128, 4], F32)
        lnscr_t = sb("lnscr", [128, 2], F32)
        glr_t = sb("glrw", [128, 8, 16], BF16)
        wgu_t = sb("wgu", [32, 512], F32)
        mix_t = sb("mixs", [128, 11776], F32)

        act_all = actp_t[:, :].bitcast(BF16).rearrange("p (c t) -> p c t", t=TT)
        oT_all = actp_t[:, 0:8 * TT].rearrange("p (c t) -> p c t", t=TT)

        def xT(c, sl=slice(None)):
            return V(xT_t[:, c, sl], [("xT", c)])

        def hT(c, sl=slice(None)):
            return V(hT_t[:, c, sl], [("hT", c)])

        def act(c, sl=slice(None)):
            return V(act_all[:, c, sl], [("act", c)])

        def sq(c):
            return act(c)

        def oT(c, sl=slice(None)):
            return V(oT_all[:, c, sl], [("act", 2 * c), ("act", 2 * c + 1)])

        def tmp(i, sl=slice(None)):
            return V(tmp_t[:, i, sl], [("tmp", i)])

        def tmpbf(i, n=TT):
            return V(tmp_t[:, i, :].bitcast(BF16)[:, 0:n], [("tmp", i)])

        def col(i, rows=slice(None)):
            return V(cols_t[rows, i:i + 1], [("cols",)])

        def cbf(c0, n, rows=slice(None)):
            return V(cbf_t[rows, c0:c0 + n], [("cbf",)])

        ident = V(ident_t[:, :], [("ident",)])
        tri = V(tri_t[:, :], [("tri",)])

        mix_bf = mix_t[:, :].bitcast(BF16)
        _off = [0]

        def carve_bf(n):
            o = _off[0]
            _off[0] += n
            return mix_bf[:, o:o + n]

        def carve_f32(n):
            assert _off[0] % 2 == 0
            o = _off[0] // 2
            _off[0] += 2 * n
            return mix_t[:, o:o + n]

        _off[0] = 0
        qTr_ap = carve_bf(NQ * 4 * 128).rearrange("p (b j t) -> p b j t", b=NQ, j=4)
        kTr_ap = carve_bf(5 * 128).rearrange("p (b t) -> p b t", b=5)
        Vp_ap = carve_bf(5 * 256).rearrange("p (b t) -> p b t", b=5)
        E_ap = carve_bf(12 * TT).rearrange("p (e t) -> p e t", e=12)
        convo_ap = carve_bf(4 * TT).rearrange("p (c t) -> p c t", c=4)
        attno_ap = carve_bf(4 * TT).rearrange("p (c t) -> p c t", c=4)
        m_ap = carve_f32(4 * 516).rearrange("p (c t) -> p c t", c=4)
        l0_end = _off[0]
        _off[0] = 0
        qtil_ap = carve_bf(4 * TT).rearrange("p (h t) -> p h t", h=4)
        ktil_ap = carve_bf(4 * TT).rearrange("p (h t) -> p h t", h=4)
        ktok_ap = carve_bf(NQ * 512).rearrange("p (b d) -> p b d", b=NQ)
        vtok_ap = carve_bf(NQ * 1024).rearrange("p (b d) -> p b d", b=NQ)
        sg_ap = carve_bf(8 * TT).rearrange("p (c t) -> p c t", c=8)
        scT_ap = carve_bf(2 * TT).rearrange("p (e t) -> p e t", e=2)
        osq_ap = carve_bf(4 * TT).rearrange("p (e t) -> p e t", e=4)
        nl_ap = carve_f32(NQ * 512).rearrange("p (b d) -> p b d", b=NQ)
        glrT_ap = carve_f32(TT)
        l1_end = _off[0]
        assert max(l0_end, l1_end) <= 23552, (l0_end, l1_end)
        S_t = sb("Sst", [128, 4, 256], F32)
        Sbf_t = sb("Sbf", [128, 2, 4, 256], BF16)
        egend_t = sb("egend", [128, 4, NQ], F32)
        kcar_t = sb("kcar", [128, 128], BF16)
        vcar_t = sb("vcar", [128, 256], BF16)
        mcar_t = sb("mcar", [128, 4, 2], F32)

        MIX = ("mix",)

        psum = [st.enter_context(nc.psum_tensor("ps%d" % i, [128, TT], F32)) for i in range(8)]
        _psi = [0]

        _reserved = set()
        _alloc_t = {}
        _alloc_n = [0]

        def newps():
            last = {i: -1 for i in range(8)}
            for o in reversed(P.ops[-400:]):
                for k in o.reads + o.writes:
                    if isinstance(k, tuple) and len(k) == 2 and k[0] == "ps" and last[k[1]] < 0:
                        last[k[1]] = o.idx
                if all(v >= 0 for v in last.values()):
                    break
            cands = [i for i in range(8) if i not in _reserved]
            b = min(cands, key=lambda i: (max(last[i], _alloc_t.get(i, -1)), (i - _psi[0]) % 8))
            _psi[0] = b + 1
            _alloc_n[0] += 1
            _alloc_t[b] = len(P.ops) + _alloc_n[0] * 1e-6
            return b

        def PS(i, sl=slice(None), rows=slice(None)):
            return V(psum[i][rows, sl], [("ps", i)])

        def mm(out, lhsT, rhs, start=True, stop=True):
            P.pe(lambda e: e.matmul(out.ap, lhsT.ap, rhs.ap, start=start, stop=stop),
                 reads=lhsT.keys + rhs.keys, writes=out.keys)

        def tr(out, in_):
            P.pe(lambda e: e.transpose(out.ap, in_.ap, ident.ap), reads=in_.keys + ident.keys, writes=out.keys)

        def actf(out, in_, func, bias=None, scale=None, eng="act"):
            kw = {}
            rk = in_.keys
            if bias is not None:
                kw["bias"] = _a(bias)
                rk = rk + _k(bias)
            if scale is not None:
                kw["scale"] = _a(scale)
                rk = rk + _k(scale)
            P.act(lambda e: e.activation(out.ap, in_.ap, func, **kw), reads=rk, writes=out.keys)

        def tt_(out, a, b, op):
            P.dve(lambda e: e.tensor_tensor(out.ap, a.ap, b.ap, op), reads=a.keys + b.keys, writes=out.keys)

        def ts_(out, a, s1, s2, op0, op1=None):
            rk = a.keys + _k(s1) + _k(s2)
            if op1 is None:
                P.dve(lambda e: e.tensor_scalar(out.ap, a.ap, _a(s1), None, op0), reads=rk, writes=out.keys)
            else:
                P.dve(lambda e: e.tensor_scalar(out.ap, a.ap, _a(s1), _a(s2), op0, op1), reads=rk, writes=out.keys)

        def stt_(out, a, s, b, op0, op1):
            P.dve(lambda e: e.scalar_tensor_tensor(out.ap, a.ap, _a(s), b.ap, op0, op1),
                  reads=a.keys + _k(s) + b.keys, writes=out.keys)

        def cp_dve(out, in_):
            P.dve(lambda e: e.tensor_copy(out.ap, in_.ap), reads=in_.keys, writes=out.keys)

        def cp_act(out, in_):
            actf(out, in_, AF.Copy)

        cur = {"t": 0}

        def pool_op(fn, reads, writes):
            if cur["t"] == 0:
                P.dve(fn, reads=reads, writes=writes)
            else:
                P.pool(fn, reads=reads, writes=writes)

        def preload_ln():
            actf(V(lnscr_t[:, 0:1], [("lnscr",)]), V(cols_t[:, C_SGN:C_SGN + 1], [("cols",)]), AF.Ln, bias=2.0)

        def memset(out, val, eng="dve"):
            P.op(eng, lambda e: e.memset(out.ap, val), writes=out.keys)

        def stage_slabs(stage):
            L = []
            l = int(stage[1])
            if stage == "l0mix":
                L.append(("kv", 8, 256, [(w_hy_in, 0, 8, 512, 256, 0)]))
                L.append(("q", 8, 512, [(w_hy_in, 0, 8, 0, 512, 0)]))
                L.append(("cb", 8, 512, [(w_hy_in, 0, 8, 768, 512, 0)]))
                L.append(("cc", 8, 512, [(w_hy_in, 0, 8, 1280, 512, 0)]))
                L.append(("cx", 8, 512, [(w_hy_in, 0, 8, 1792, 512, 0)]))
                L.append(("o0", 8, 512, [(w_hy_out, 0, 8, 0, 512, 0)]))
                L.append(("o1", 8, 512, [(w_hy_out, 0, 8, 512, 512, 0)]))
            elif stage == "l1mix":
                L.append(("k", 8, 512, [(w_gla_in, 0, 8, 512, 512, 0)]))
                L.append(("v0", 8, 512, [(w_gla_in, 0, 8, 1024, 512, 0)]))
                L.append(("v1", 8, 512, [(w_gla_in, 0, 8, 1536, 512, 0)]))
                L.append(("q", 8, 512, [(w_gla_in, 0, 8, 0, 512, 0)]))
                L.append(("og0", 8, 512, [(w_gla_in, 0, 8, 2064, 512, 0)]))
                L.append(("og1", 8, 512, [(w_gla_in, 0, 8, 2576, 512, 0)]))
                L.append(("o0", 8, 512, [(w_gla_out, 0, 8, 0, 512, 0)]))
                L.append(("o1", 8, 512, [(w_gla_out, 0, 8, 512, 512, 0)]))
            elif stage.endswith("ffn"):
                for i in range(11):
                    L.append(("up%d" % i, 8, 512, [(w_up[l], 0, 8, 256 * i, 256, 0), (w_up[l], 0, 8, D_FF + 256 * i, 256, 256)]))
                for g in range(4):
                    L.append(("dnA%d" % g, 11, 256, [(w_down[l], 0, 11, 256 * g, 256, 0)]))
                    L.append(("dnB%d" % g, 11, 256, [(w_down[l], 11, 11, 256 * g, 256, 0)]))
            else:
                L.append(("pp", 2, 1024, [(w_pp[l], 0, 2, 0, 1024, 0)]))
                L.append(("pg0", 8, 512, [(w_pg[l], 0, 8, 0, 512, 0)]))
                L.append(("pg1", 8, 512, [(w_pg[l], 0, 8, 512, 512, 0)]))
            return L

        slabs = []
        ws = {"issued": 0, "cursor": 0, "done": set()}
        tile_slabs = []
        wbf_box = [None]

        def slab_view(i):
            tag, KC, width, parts = slabs[i]
            return ring_t[:, i % NSLOT, 0:KC * width].rearrange("p (k n) -> p k n", k=KC)

        def prepass():
            wbf_box[0] = nc.dram_tensor("wbf", [len(tile_slabs), 128, SLOT], BF16, kind="Internal").ap()
            k = 0
            for j, (tag, KC, width, parts) in enumerate(tile_slabs):
                assert KC * width <= SLOT
                dview = wbf_box[0][j, :, 0:KC * width].rearrange("p (k n) -> p k n", k=KC)
                for (src, kr0, nkc, n0, n, doff) in parts:
                    srcv = src[kr0 * 128:(kr0 + nkc) * 128, n0:n0 + n].rearrange("(k p) n -> p k n", p=128)
                    dstv = dview[:, 0:nkc, doff:doff + n]
                    P.dma("pool", lambda e, d=dstv, s_=srcv: e.dma_start(out=d, in_=s_),
                          writes=[("wbf", j, doff)])
                    k += 1

        def pump():
            nts = len(tile_slabs)
            while ws["issued"] < len(slabs) and (ws["issued"] < NSLOT or (ws["issued"] - NSLOT) in ws["done"]):
                i = ws["issued"]
                tag, KC, width, parts = slabs[i]
                j = i % nts
                dst = ring_t[:, i % NSLOT, 0:KC * width]
                srcv = wbf_box[0][j, :, 0:KC * width]
                P.dma("sp", lambda e, d=dst, s_=srcv: e.dma_start(out=d, in_=s_),
                      reads=[("wbf", j, p[5]) for p in parts], writes=[("w", i % NSLOT)])
                ws["issued"] += 1

        def wnext(tag):
            i = ws["cursor"]
            ws["cursor"] += 1
            assert slabs[i][0] == tag, (slabs[i][0], tag)
            pump()
            assert i < ws["issued"], ("weight ring liveness too deep", tag)
            return i, slab_view(i)

        def wdone(i):
            ws["done"].add(i)
            pump()

        def W(slot, view, kc, c0, n):
            return V(view[:, kc, c0:c0 + n], [("w", slot % NSLOT)])

        P.dma("sp", lambda e: e.dma_start(out=cols_t[:, :], in_=cols_d), writes=[("cols",)])
        P.dma("sp", lambda e: e.dma_start(out=ident_t[:, :], in_=ident_d), writes=[("ident",)])
        P.dma("sp", lambda e: e.dma_start(out=tri_t[:, :], in_=tri_d), writes=[("tri",)])
        P.dma("pool", lambda e: e.dma_start(out=cbf_t[:, :], in_=cbf_d), writes=[("cbf",)])
        P.dma("sp", lambda e: e.dma_start(out=wgu_t[0:17, :], in_=wgu_d), writes=[("wgu",)])
        P.dma("pool", lambda e: e.dma_start(out=glr_t[:, :, :], in_=w_gla_in[:, 2048:2064].rearrange("(k p) n -> p k n", p=128)),
              writes=[("glrw",)])
        memset(V(tails_t[:, :, :, :], [("tails", 0), ("tails", 1)]), 0.0)
        memset(V(fix_t[:, :, :, :], [("fix", 0), ("fix", 1)]), 0.0)
        memset(V(S_t[:, :, :], [("S", h) for h in range(4)]), 0.0)
        memset(V(Sbf_t[:, :, :, :], [("Sbf", i, h) for i in range(2) for h in range(4)]), 0.0)
        memset(V(kcar_t[:, :], [("kcar",)]), 0.0)
        memset(V(vcar_t[:, :], [("vcar",)]), 0.0)
        memset(V(mcar_t[:, :, :], [("mcar",)]), 0.0)
        actf(V(sinke_t[:, :], [("sinke",)]), V(cols_t[:, C_SINK:C_SINK + 4], [("cols",)]), AF.Exp)

        onesK = cbf(B_ONESK, 128)
        blk64 = cbf(B_BLK64, 128)
        pswap = cbf(B_PSWAP, 128)
        ones256 = cbf(B_ONES256, 128)
        mown = cbf(B_MOWN, 512)
        mprev = cbf(B_MPREV, 512)

        _stt = {"bank": None, "n": 0}

        def stats_chunk(c):
            if _stt["bank"] is None:
                _stt["bank"] = newps()
                _reserved.add(_stt["bank"])
                _stt["n"] = 0
            k = _stt["n"]
            sqv = V(sqt_t[:, k % 3, :], [("sqt", k % 3)])
            actf(sqv, xT(c), AF.Square)
            if k >= 2:
                stats_mm(k - 2)
            _stt["n"] = k + 1

        def stats_mm(k):
            sqv = V(sqt_t[:, k % 3, :], [("sqt", k % 3)])
            mm(PS(_stt["bank"]), onesK, sqv, start=(k == 0), stop=(k == 7))

        def norm_finish(gcol0):
            assert _stt["n"] == 8
            stats_mm(6)
            stats_mm(7)
            ss = _stt["bank"]
            _reserved.discard(ss)
            _stt["bank"] = None
            _stt["n"] = 0
            lnv = tmp(0)
            rstd = tmp(1)
            actf(lnv, PS(ss), AF.Ln, bias=RMS_EPS)
            actf(rstd, lnv, AF.Exp, scale=-0.5)
            for c in range(8):
                stt_(hT(c), xT(c), col(gcol0 + c), rstd, ALU.mult, ALU.mult)

        def proj_fm(slot, view, c0, src, nk, bank=None):
            b = newps() if bank is None else bank
            for kc in range(nk):
                mm(PS(b), W(slot, view, kc, c0, 128), src(kc), start=(kc == 0), stop=(kc == nk - 1))
            return b

        def resid_add(oc, b, stats):
            tt_(xT(oc), xT(oc), PS(b), ALU.add)
            finish_chunk(oc, stats)

        xstg = xin_t[:, :, :].rearrange("p a (b f) -> p (a b) f", b=2)
        ystg = xout_t[:, 1, :].rearrange("p (a f) -> p a f", a=2)

        def x_dma(t, c):
            buf = c % 4
            P.dma("sp", lambda e: e.dma_start(
                out=xstg[:, buf, :].rearrange("p (b f) -> p b f", b=NQ),
                in_=x_d[t * TT:(t + 1) * TT, c * 128:(c + 1) * 128].rearrange("(b p) f -> p b f", p=128)),
                writes=[("xin", buf)])

        def load_chunk(t, c):
            buf = c % 4
            b = newps()
            for blk in range(NQ):
                tr(PS(b, slice(blk * 128, (blk + 1) * 128)), V(xstg[:, buf, blk * 128:(blk + 1) * 128], [("xin", buf)]))
            if c % 2 == 0:
                cp_act(xT(c), PS(b))
            else:
                cp_dve(xT(c), PS(b))
            if c + 4 < 8:
                x_dma(t, c + 4)
            stats_chunk(c)

        def store_chunk(t, c):
            buf = c % 2
            b = newps()
            for blk in range(NQ):
                tr(PS(b, slice(blk * 128, (blk + 1) * 128)), xT(c, slice(blk * 128, (blk + 1) * 128)))
            yv = V(ystg[:, buf, :], [("xout", 1, buf)])
            if c % 2 == 0:
                cp_dve(yv, PS(b))
            else:
                cp_act(yv, PS(b))
            P.dma("sp", lambda e: e.dma_start(
                out=y_d[t * TT:(t + 1) * TT, c * 128:(c + 1) * 128].rearrange("(b p) f -> p b f", p=128),
                in_=ystg[:, buf, :].rearrange("p (b f) -> p b f", b=NQ)),
                reads=[("xout", 1, buf)], writes=[("y", t, c)])

        fin = {"last": False, "t": 0}

        def finish_chunk(oc, stats):
            if not fin["last"]:
                if stats:
                    stats_chunk(oc)
                return
            if fin.get("pend") is not None:
                pc = fin["pend"]
                store_chunk(fin["t"], pc)
                if fin["t"] + 1 < NT:
                    load_chunk(fin["t"] + 1, pc)
            fin["pend"] = oc

        def finish_flush():
            if fin.get("pend") is not None:
                pc = fin["pend"]
                store_chunk(fin["t"], pc)
                if fin["t"] + 1 < NT:
                    load_chunk(fin["t"] + 1, pc)
                fin["pend"] = None

        def p_dma(l, t):
            P.dma("sp", lambda e: e.dma_start(out=pin_ap, in_=p_d[l, t * TT:(t + 1) * TT, :].rearrange("(b q) d -> q b d", q=128)),
                  writes=[PIN_KEY])

        def p_transposes():
            for pc in range(2):
                b = newps()
                for blk in range(NQ):
                    tr(PS(b, slice(blk * 128, (blk + 1) * 128)), V(pin_ap[:, blk, pc * 128:(pc + 1) * 128], [PIN_KEY]))
                cp_act(V(pT_t[:, pc, :], [("pT", pc)]), PS(b))

        def ffn(l, t, stats, mid_hook=None):
            norm_finish(C_FFNN + l * 8)
            p_transposes()
            for i in range(11):
                slot, view = wnext("up%d" % i)
                for ci in range(2):
                    c = 2 * i + ci
                    bg = proj_fm(slot, view, ci * 128, hT, 8)
                    bu = proj_fm(slot, view, 256 + ci * 128, hT, 8)
                    ys = []
                    for which, b, cidx in (("g", bg, c), ("u", bu, 22 + c)):
                        ti = 2 + (0 if which == "g" else 1) + 2 * (c % 2)
                        y = tmp(ti)
                        w0 = col(C_CW + (l * 3 + 0) * 44 + cidx)
                        w1 = col(C_CW + (l * 3 + 1) * 44 + cidx)
                        w2 = col(C_CW + (l * 3 + 2) * 44 + cidx)
                        bb = col(C_CB + l * 44 + cidx)
                        actf(y, PS(b), AF.Identity, bias=bb, scale=w2)
                        stt_(tmp(ti, slice(1, TT)), PS(b, slice(0, TT - 1)), w1, tmp(ti, slice(1, TT)), ALU.mult, ALU.add)
                        stt_(tmp(ti, slice(2, TT)), PS(b, slice(0, TT - 2)), w0, tmp(ti, slice(2, TT)), ALU.mult, ALU.add)
                        fixv = V(fix_t[:, l, cidx, :], [("fix", l)])
                        pool_op(lambda e, ti=ti, fixv=fixv: e.tensor_tensor(tmp_t[:, ti, 0:2], tmp_t[:, ti, 0:2], fixv.ap, ALU.add),
                                reads=[("tmp", ti)] + list(fixv.keys), writes=[("tmp", ti)])
                        cp_act(V(tails_t[:, l, cidx, :], [("tails", l)]), PS(b, slice(TT - 2, TT)))
                        ys.append(ti)
                    gi = 6 + (c % 2)
                    actf(tmp(gi), tmp(ys[0]), AF.Gelu)
                    pool_op(lambda e, c=c, gi=gi, yi=ys[1]: e.tensor_tensor(act_all[:, c, :], tmp_t[:, gi, :], tmp_t[:, yi, :], ALU.mult),
                            reads=[("tmp", gi), ("tmp", ys[1])], writes=[("act", c)])
                wdone(slot)
            tl = lambda i: V(tails_t[:, l, :, i], [("tails", l)])
            fx = lambda i: V(fix_t[:, l, :, i], [("fix", l)])
            cw = lambda k: V(cols_t[:, C_CW + (l * 3 + k) * 44:C_CW + (l * 3 + k) * 44 + 44], [("cols",)])
            ftmp = V(fixtmp_t[:, :], [("fixtmp",)])
            tt_(fx(1), tl(1), cw(0), ALU.mult)
            tt_(fx(0), tl(0), cw(0), ALU.mult)
            tt_(ftmp, tl(1), cw(1), ALU.mult)
            tt_(fx(0), fx(0), ftmp, ALU.add)
            preload_ln() if mid_hook is None else None
            sl = {}
            for g in range(2):
                sl[("A", g)] = wnext("dnA%d" % g)
                sl[("B", g)] = wnext("dnB%d" % g)
            acc = {(g, oi): newps() for g in range(2) for oi in range(2)}
            for b in acc.values():
                _reserved.add(b)
            for g in range(2):
                sa, va = sl[("A", g)]
                for oi in range(2):
                    for kc in range(11):
                        mm(PS(acc[(g, oi)]), W(sa, va, kc, oi * 128, 128), act(kc), start=(kc == 0), stop=False)
                wdone(sa)
            for g in range(2):
                sb_, vb = sl[("B", g)]
                for oi in range(2):
                    for kc in range(9):
                        mm(PS(acc[(g, oi)]), W(sb_, vb, kc, oi * 128, 128), act(11 + kc), start=False, stop=False)
            for g in range(2):
                sb_, vb = sl[("B", g)]
                for oi in range(2):
                    for kc in range(9, 11):
                        mm(PS(acc[(g, oi)]), W(sb_, vb, kc, oi * 128, 128), act(11 + kc), start=False, stop=(kc == 10))
                    _reserved.discard(acc[(g, oi)])
                    resid_add(g * 2 + oi, acc[(g, oi)], stats)
                wdone(sb_)
            for g in range(2, 4):
                if g == 2 and mid_hook is not None:
                    mid_hook()
                    preload_ln()
                sa, va = wnext("dnA%d" % g)
                sb_, vb = wnext("dnB%d" % g)
                b0, b1 = newps(), newps()
                for (b, oi) in ((b0, 0), (b1, 1)):
                    for kc in range(11):
                        mm(PS(b), W(sa, va, kc, oi * 128, 128), act(kc), start=(kc == 0), stop=False)
                wdone(sa)
                for (b, oi) in ((b0, 0), (b1, 1)):
                    for kc in range(11):
                        mm(PS(b), W(sb_, vb, kc, oi * 128, 128), act(11 + kc), start=False, stop=(kc == 10))
                    resid_add(g * 2 + oi, b, stats)
                wdone(sb_)

        def ple(l, t, stats):
            norm_finish(C_PLEN + l * 8)
            spp, vpp = wnext("pp")
            sg0, vg0 = wnext("pg0")
            sg1, vg1 = wnext("pg1")
            pTv = lambda pc: V(pT_t[:, pc, :], [("pT", pc)])
            for oc in range(8):
                bp = proj_fm(spp, vpp, oc * 128, pTv, 2)
                if oc < 4:
                    bg = proj_fm(sg0, vg0, oc * 128, hT, 8)
                else:
                    bg = proj_fm(sg1, vg1, (oc - 4) * 128, hT, 8)
                if oc == 3:
                    wdone(sg0)
                if oc == 7:
                    wdone(spp)
                    wdone(sg1)
                gate = tmp(2 + (oc % 2))
                actf(gate, PS(bg), AF.Sigmoid)
                if oc == 7:
                    preload_ln()
                prod = tmp(4 + (oc % 2))
                tt_(prod, gate, PS(bp), ALU.mult)
                tt_(xT(oc), xT(oc), prod, ALU.add)
                finish_chunk(oc, stats)

        L0_KEYS = [("qTr",), ("kTr",), ("Vp",)] + [("E", i) for i in range(12)] + [("convo", c) for c in range(4)] + \
                  [("attno",)] + [("m", c) for c in range(4)]
        L1_KEYS = [("qtil", h) for h in range(4)] + [("ktil", h) for h in range(4)] + [("ktok", b) for b in range(NQ)] + \
                  [("vtok", b) for b in range(NQ)] + [("sg", c) for c in range(8)] + [("scT", i) for i in range(2)] + \
                  [("osq", i) for i in range(4)] + [("nl", b) for b in range(NQ)] + [("glrT",)]

        def fence(keys):
            for e in ("pe", "act", "dve"):
                o = P.op(e, None, reads=keys)
                o.fence = True

        sin_v = V(cs_t[:, 1, :], [("sin",)])
        cos_v = V(cs_t[:, 0, :], [("cos",)])

        def rope_tables(t):
            posi = V(posi_t[:, :], [("posi",)])
            ang, kf, r = tmp(2), tmp(3), tmp(4)
            ki = V(tmp_t[:, 5, :].bitcast(I32), [("tmp", 5)])
            cp_dve(ang, posi)
            ts_(ang, ang, col(C_INVF), None, ALU.mult)
            ts_(kf, ang, 1.0 / TWO_PI, None, ALU.mult)
            cp_dve(ki, kf)
            cp_dve(kf, ki)
            stt_(r, kf, -CW_C1, ang, ALU.mult, ALU.add)
            stt_(r, kf, -CW_C2, r, ALU.mult, ALU.add)
            rs = tmp(6)
            ts_(rs, r, 3.1415925, -3.1415925, ALU.min, ALU.max)
            actf(sin_v, rs, AF.Sin)
            ts_(sin_v, sin_v, col(C_SGN), None, ALU.mult)
            rc, g = tmp(7), tmp(8)
            ts_(rc, r, 1.5707963267948966, None, ALU.add)
            ts_(g, rc, 3.141592653589793, TWO_PI, ALU.is_gt, ALU.mult)
            tt_(rc, rc, g, ALU.subtract)
            ts_(rc, rc, 3.1415925, -3.1415925, ALU.min, ALU.max)
            actf(cos_v, rc, AF.Sin)

        def pos_dma(t):
            P.dma("sp", lambda e: e.dma_start(out=posi_t[:, :], in_=pos_d[0:1, t * TT:(t + 1) * TT].partition_broadcast(128)),
                  writes=[("posi",)])

        def l0_mixer(t, stats):
            fence(L1_KEYS)
            norm_finish(C_MIXN + 0)
            Vp = lambda blk, g: V(Vp_ap[:, blk, g * 128:(g + 1) * 128], [("Vp",)])
            memset(V(Vp_ap[:, :, :], [("Vp",)]), 0.0)
            cp_dve(V(kTr_ap[:, 0, :], [("kTr",)]), V(kcar_t[:, :], [("kcar",)]))
            cp_dve(V(Vp_ap[:, 0, :], [("Vp",)]), V(vcar_t[:, :], [("vcar",)]))
            cp_dve(V(m_ap[:, :, 0:2], [("m", c) for c in range(4)]), V(mcar_t[:, :, :], [("mcar",)]))
            skv, vkv = wnext("kv")
            sq_, vq_ = wnext("q")
            scb, vcb = wnext("cb")
            scc, vcc = wnext("cc")
            scx, vcx = wnext("cx")
            banks = {}

            def qk_A(j):
                if j == 4:
                    banks[("A", j)] = proj_fm(skv, vkv, 0, hT, 8)
                else:
                    banks[("A", j)] = proj_fm(sq_, vq_, j * 128, hT, 8)
                _reserved.add(banks[("A", j)])

            def qk_B_ew(j):
                s0 = 2 + 4 * (items.index(j) % 2)
                sqh = V(tmp_t[:, s0, :].bitcast(BF16)[:, 0:TT], [("tmp", s0)])
                actf(sqh, PS(banks[("A", j)]), AF.Square)

            def qk_B_pe(j):
                s0 = 2 + 4 * (items.index(j) % 2)
                sqh = V(tmp_t[:, s0, :].bitcast(BF16)[:, 0:TT], [("tmp", s0)])
                b2 = newps()
                mm(PS(b2), blk64, sqh)
                banks[("B", j)] = b2
                _reserved.add(b2)

            def qk_C_ew(j):
                s0 = 2 + 4 * (items.index(j) % 2)
                rstd = tmp(s0 + 1)
                actf(rstd, PS(banks[("B", j)]), AF.Ln, bias=RMS_EPS)
                actf(rstd, rstd, AF.Exp, scale=-0.5)
                qn = V(tmp_t[:, s0, :].bitcast(BF16)[:, TT:2 * TT], [("tmp", s0)])
                stt_(qn, PS(banks[("A", j)]), col(C_QN if j < 4 else C_KN), rstd, ALU.mult, ALU.mult)
                _reserved.discard(banks[("A", j)])
                _reserved.discard(banks[("B", j)])

            def qk_C_pe(j):
                s0 = 2 + 4 * (items.index(j) % 2)
                qn = V(tmp_t[:, s0, :].bitcast(BF16)[:, TT:2 * TT], [("tmp", s0)])
                b3 = newps()
                mm(PS(b3), pswap, qn)
                banks[("C", j)] = b3
                _reserved.add(b3)

            def qk_D(j):
                s0 = 2 + 4 * (items.index(j) % 2)
                qn = V(tmp_t[:, s0, :].bitcast(BF16)[:, TT:2 * TT], [("tmp", s0)])
                t1, t2 = tmp(s0 + 2), tmp(s0 + 3)
                tt_(t1, qn, cos_v, ALU.mult)
                tt_(t2, PS(banks[("C", j)]), sin_v, ALU.mult)
                if j < 4:
                    outv = V(qTr_ap[:, :, j, :], [("qTr",)])
                else:
                    outv = V(kTr_ap[:, 1:5, :], [("kTr",)])
                P.dve(lambda e: e.tensor_tensor(outv.ap, tmp_t[:, s0 + 2, :].rearrange("p (b t) -> p b t", b=NQ),
                                                tmp_t[:, s0 + 3, :].rearrange("p (b t) -> p b t", b=NQ), ALU.add),
                      reads=t1.keys + t2.keys, writes=outv.keys)
                _reserved.discard(banks[("C", j)])

            def v_proj():
                bv = newps()
                for blk in range(NQ):
                    for kc in range(8):
                        mm(PS(bv, slice(blk * 128, (blk + 1) * 128)), hT(kc, slice(blk * 128, (blk + 1) * 128)),
                           W(skv, vkv, kc, 128, 128), start=(kc == 0), stop=(kc == 7))
                pv4 = psum[bv][:, :].rearrange("p (b g d) -> p b g d", b=NQ, g=2)
                P.act(lambda e: e.activation(Vp_ap[:, 1:5, 0:64], pv4[:, :, 0, :], AF.Copy), reads=[("ps", bv)], writes=[("Vp",)])
                P.dve(lambda e: e.tensor_copy(Vp_ap[:, 1:5, 192:256], pv4[:, :, 1, :]), reads=[("ps", bv)], writes=[("Vp",)])

            cbanks = {}

            def conv_pe(c):
                cbanks[c] = (proj_fm(scc, vcc, c * 128, hT, 8), proj_fm(scx, vcx, c * 128, hT, 8),
                             proj_fm(scb, vcb, c * 128, hT, 8))
                for b in cbanks[c]:
                    _reserved.add(b)

            def conv_ew(c):
                b_cc, b_cx, b_cb = cbanks[c]
                mk = [("m", c)]
                mcur = V(m_ap[:, c, 2:2 + TT], mk)
                cp_act(mcur, PS(b_cc))
                tt_(mcur, mcur, PS(b_cx), ALU.mult)
                y = tmp(c % 2)
                ts_(y, mcur, col(C_HYCW + 2 * 4 + c), None, ALU.mult)
                stt_(y, V(m_ap[:, c, 1:1 + TT], mk), col(C_HYCW + 1 * 4 + c), y, ALU.mult, ALU.add)
                stt_(y, V(m_ap[:, c, 0:TT], mk), col(C_HYCW + 0 * 4 + c), y, ALU.mult, ALU.add)
                tt_(V(convo_ap[:, c, :], [("convo", c)]), y, PS(b_cb), ALU.mult)
                cp_act(V(mcar_t[:, c, :], [("mcar",)]), V(m_ap[:, c, TT:TT + 2], mk))
                for b in cbanks[c]:
                    _reserved.discard(b)

            items = [4, 0, 1, 2, 3]
            for step in range(8):
                if 0 <= step - 2 < 5:
                    qk_C_ew(items[step - 2])
                if 0 <= step - 1 < 5:
                    qk_B_ew(items[step - 1])
                if step < 5:
                    qk_A(items[step])
                if step == 0:
                    pass
                if step == 6:
                    v_proj()
                    wdone(skv)
                if step == 4:
                    wdone(sq_)
                if step >= 2 and step - 2 < 4:
                    conv_pe(step - 2)
                if 0 <= step - 1 < 5:
                    qk_B_pe(items[step - 1])
                if 0 <= step - 2 < 5:
                    qk_C_pe(items[step - 2])
                if 0 <= step - 3 < 5:
                    qk_D(items[step - 3])
                if step >= 2 and step - 2 < 4:
                    conv_ew(step - 2)
            wdone(scb)
            wdone(scc)
            wdone(scx)
            so0, vo0 = wnext("o0")
            so1, vo1 = wnext("o1")
            osrc = lambda kc: V(attno_ap[:, kc, :], [("attno",)]) if kc < 4 else V(convo_ap[:, kc - 4, :], [("convo", kc - 4)])
            opre = {}
            for oc in range(3):
                b = newps()
                _reserved.add(b)
                for kc in (4, 5, 6, 7):
                    mm(PS(b), W(so0, vo0, kc, oc * 128, 128), osrc(kc), start=(kc == 4), stop=False)
                opre[oc] = b
            Eb = {}

            def attn_scores(n):
                gblk = t * NQ + n
                Es = []
                for g in range(2):
                    rows = slice(64 * g, 64 * g + 64)
                    for which in (("own", n + 1, mown), ("prev", n, mprev)):
                        if which[0] == "prev" and gblk == 0:
                            continue
                        b = newps()
                        P.pe(lambda e, b=b, rows=rows, kb=which[1], n=n: e.matmul(
                            psum[b][:, :], kTr_ap[rows, kb, :], qTr_ap[rows, n, :, :].rearrange("p j t -> p (j t)"),
                            start=True, stop=True), reads=[("kTr",), ("qTr",)], writes=[("ps", b)])
                        ei = (n % 3) * 4 + len(Es)
                        Ev = V(E_ap[:, ei, :], [("E", ei)])
                        actf(Ev, PS(b), AF.Exp, scale=0.125)
                        if g == 0:
                            pool_op(lambda e, Ev=Ev, mk=which[2]: e.tensor_tensor(Ev.ap, Ev.ap, mk.ap, ALU.mult),
                                    reads=Ev.keys + which[2].keys, writes=Ev.keys)
                        else:
                            tt_(Ev, Ev, which[2], ALU.mult)
                        Es.append((Ev, g, which[1]))
                Eb[n] = Es

            def attn_finish(n):
                Es = Eb[n]
                bo = newps()
                for i, (Ev, g, kb) in enumerate(Es):
                    mm(PS(bo), Vp(kb, g), Ev, start=(i == 0), stop=(i == len(Es) - 1))
                bd = newps()
                for i, (Ev, g, kb) in enumerate(Es):
                    mm(PS(bd), cbf(B_ONESPAD + g * 128, 128), Ev, start=(i == 0), stop=(i == len(Es) - 1))
                rden = tmp(2 + (n % 2))
                for j in range(4):
                    jsl = slice(j * 128, (j + 1) * 128)
                    actf(tmp(2 + (n % 2), jsl), PS(bd, jsl), AF.Ln, bias=V(sinke_t[:, j:j + 1], [("sinke",)]))
                actf(rden, rden, AF.Exp, scale=-1.0)
                P.dve(lambda e, bo=bo, rden=rden, n=n: e.tensor_tensor(
                    attno_ap[:, :, n * 128:(n + 1) * 128], psum[bo][:, :].rearrange("p (j t) -> p j t", j=4),
                    rden.ap.rearrange("p (j t) -> p j t", j=4), ALU.mult), reads=[("ps", bo)] + list(rden.keys), writes=[("attno",)])

            for n in range(NQ + 2):
                if n < NQ:
                    attn_scores(n)
                if n - 2 >= 0:
                    attn_finish(n - 2)
            cp_act(V(kcar_t[:, :], [("kcar",)]), V(kTr_ap[:, 4, :], [("kTr",)]))
            cp_act(V(vcar_t[:, :], [("vcar",)]), V(Vp_ap[:, 4, :], [("Vp",)]))
            for oc in range(3, 7):
                b = newps()
                _reserved.add(b)
                for kc in (4, 5, 6, 7):
                    if oc < 4:
                        wv = W(so0, vo0, kc, oc * 128, 128)
                    else:
                        wv = W(so1, vo1, kc, (oc - 4) * 128, 128)
                    mm(PS(b), wv, osrc(kc), start=(kc == 4), stop=False)
                opre[oc] = b
            for oc in range(8):
                if oc in opre:
                    b = opre[oc]
                    kcs = (0, 1, 2, 3)
                else:
                    b = newps()
                    kcs = (4, 5, 6, 7, 0, 1, 2, 3)
                for kc in kcs:
                    if oc < 4:
                        wv = W(so0, vo0, kc, oc * 128, 128)
                    else:
                        wv = W(so1, vo1, kc, (oc - 4) * 128, 128)
                    mm(PS(b), wv, osrc(kc), start=(kc == 4 and oc not in opre), stop=(kc == 3))
                _reserved.discard(b)
                resid_add(oc, b, stats)
                if oc == 3:
                    wdone(so0)
            wdone(so1)

        def l1_mixer(t, stats):
            fence(L0_KEYS)
            norm_finish(C_MIXN + 8)
            qtil = lambda h, sl=slice(None): V(qtil_ap[:, h, sl], [("qtil", h)])
            ktil = lambda h, sl=slice(None): V(ktil_ap[:, h, sl], [("ktil", h)])
            memset(V(glrT_ap[0:32, :], [("glrT",)]), 1.0)
            bgl = newps()
            for kc in range(8):
                mm(PS(bgl, rows=slice(0, 16)), V(glr_t[:, kc, :], [("glrw",)]), hT(kc), start=(kc == 0), stop=(kc == 7))
            cp_dve(V(glrT_ap[0:16, :], [("glrT",)]), PS(bgl, rows=slice(0, 16)))
            wgu = V(wgu_t[0:17, :], [("wgu",)])
            sk_, vk_ = wnext("k")
            sv0, vv0 = wnext("v0")
            sv1, vv1 = wnext("v1")
            for n in range(NQ):
                bs = slice(n * 128, (n + 1) * 128)
                bx = newps()
                mm(PS(bx), V(glrT_ap[0:17, bs], [("glrT",)]), wgu)
                ex = tmp(2 + (n % 2))
                actf(ex, PS(bx), AF.Exp, scale=-1.0)
                nlv = V(nl_ap[:, n, :], [("nl", n)])
                actf(nlv, ex, AF.Ln, bias=1.0)
                for half in range(2):
                    bv = newps()
                    for kc in range(8):
                        mm(PS(bv), hT(kc, bs), W(sv0 if half == 0 else sv1, vv0 if half == 0 else vv1, kc, 0, 512),
                           start=(kc == 0), stop=(kc == 7))
                    vv = V(vtok_ap[:, n, half * 512:(half + 1) * 512], [("vtok", n)])
                    if half == 0:
                        cp_act(vv, PS(bv))
                    else:
                        cp_dve(vv, PS(bv))
            wdone(sv0)
            wdone(sv1)
            sq_, vq_ = wnext("q")
            for h in range(4):
                bq = proj_fm(sq_, vq_, h * 128, hT, 8)
                bk = proj_fm(sk_, vk_, h * 128, hT, 8)
                bat = newps()
                for n in range(NQ):
                    mm(PS(bat, slice(n * 128, (n + 1) * 128)), V(nl_ap[:, n, h * 128:(h + 1) * 128], [("nl", n)]), tri)
                eg, en = tmp(2 + (h % 2)), tmp(4 + (h % 2))
                actf(eg, PS(bat), AF.Exp, scale=-1.0)
                actf(en, PS(bat), AF.Exp)
                P.dve(lambda e, h=h, eg=eg: e.tensor_copy(egend_t[:, h, :], eg.ap.rearrange("p (b t) -> p b t", b=NQ)[:, :, 127]),
                      reads=eg.keys, writes=[("egend",)])
                stt_(qtil(h), PS(bq), 128.0 ** -0.5, eg, ALU.mult, ALU.mult)
                tt_(ktil(h), PS(bk), en, ALU.mult)
            wdone(sk_)
            wdone(sq_)
            identb = cbf(B_IDENT, 128)
            for n in range(NQ):
                bt = newps()
                pbf = psum[bt][:, :].bitcast(BF16)
                for h in range(4):
                    P.pe(lambda e, pbf=pbf, h=h, n=n: e.transpose(pbf[:, h * 128:(h + 1) * 128], ktil_ap[:, h, n * 128:(n + 1) * 128], identb.ap),
                         reads=[("ktil", h)] + list(identb.keys), writes=[("ps", bt)])
                P.act(lambda e, pbf=pbf, n=n: e.activation(ktok_ap[:, n, :], pbf[:, 0:512], AF.Copy),
                      reads=[("ps", bt)], writes=[("ktok", n)])
            sg0, vg0 = wnext("og0")
            sg1, vg1 = wnext("og1")

            ogb = {}

            def og_pe(c):
                if c < 4:
                    b = proj_fm(sg0, vg0, c * 128, hT, 8)
                else:
                    b = proj_fm(sg1, vg1, (c - 4) * 128, hT, 8)
                ogb[c] = b
                _reserved.add(b)
                if c == 3:
                    wdone(sg0)
                if c == 7:
                    wdone(sg1)

            def og_act(c):
                actf(V(sg_ap[:, c, :], [("sg", c)]), PS(ogb[c]), AF.Silu)
                _reserved.discard(ogb[c])
                if c == 7:
                    preload_ln()

            def og_chunk(c):
                og_pe(c)
                og_act(c)

            def sc_block(n):
                bs = slice(n * 128, (n + 1) * 128)
                bsc = newps()
                for h in range(4):
                    mm(PS(bsc, slice(h * 128, (h + 1) * 128)), ktil(h, bs), qtil(h, bs))
                tt_(V(scT_ap[:, n % 2, :], [("scT", n % 2)]), PS(bsc), mown, ALU.mult)

            def state_block(n):
                for hh in range(2):
                    bsd = newps()
                    for hi in range(2):
                        h = hh * 2 + hi
                        mm(PS(bsd, slice(hi * 256, (hi + 1) * 256)), V(ktok_ap[:, n, h * 128:(h + 1) * 128], [("ktok", n)]),
                           V(vtok_ap[:, n, h * 256:(h + 1) * 256], [("vtok", n)]))
                    for hi in range(2):
                        h = hh * 2 + hi
                        Sv = V(S_t[:, h, :], [("S", h)])
                        ecol = V(egend_t[:, h, n:n + 1], [("egend",)])
                        ts_(Sv, Sv, ecol, None, ALU.mult)
                        stt_(Sv, PS(bsd, slice(hi * 256, (hi + 1) * 256)), ecol, Sv, ALU.mult, ALU.add)
                        cp_act(V(Sbf_t[:, (n + 1) % 2, h, :], [("Sbf", (n + 1) % 2, h)]), Sv)

            def o_block(n):
                bs = slice(n * 128, (n + 1) * 128)
                for hh in range(2):
                    bo = newps()
                    for hi in range(2):
                        h = hh * 2 + hi
                        for vh in range(2):
                            osl = slice((hi * 2 + vh) * 128, (hi * 2 + vh + 1) * 128)
                            mm(PS(bo, osl), V(Sbf_t[:, n % 2, h, vh * 128:(vh + 1) * 128], [("Sbf", n % 2, h)]), qtil(h, bs),
                               start=True, stop=False)
                            mm(PS(bo, osl), V(vtok_ap[:, n, h * 256 + vh * 128:h * 256 + (vh + 1) * 128], [("vtok", n)]),
                               V(scT_ap[:, n % 2, h * 128:(h + 1) * 128], [("scT", n % 2)]), start=False, stop=True)
                    outv = V(oT_all[:, hh * 4:hh * 4 + 4, bs], [("act", i) for i in range(hh * 8, hh * 8 + 8)])
                    inv = V(psum[bo][:, :].rearrange("p (c t) -> p c t", c=4), [("ps", bo)])
                    if hh == 0:
                        cp_act(outv, inv)
                    else:
                        cp_dve(outv, inv)

            sc_block(0)
            for n in range(NQ):
                state_block(n)
                if n < NQ - 1:
                    og_chunk(2 * n)
                if n + 1 < NQ:
                    sc_block(n + 1)
                if n < NQ - 1:
                    og_chunk(2 * n + 1)
                o_block(n)
            og_pe(6)
            og_pe(7)
            so0, vo0 = wnext("o0")
            so1, vo1 = wnext("o1")
            obank = [newps() for _ in range(4)]
            for b in obank:
                _reserved.add(b)
            def osq(h, vh):
                i = (h % 2) * 2 + vh
                return V(osq_ap[:, i, :], [("osq", i)])

            for vh in range(2):
                actf(osq(0, vh), oT(vh), AF.Square)
            for h in range(4):
                if h + 1 < 4:
                    for vh in range(2):
                        actf(osq(h + 1, vh), oT(2 * (h + 1) + vh), AF.Square)
                if h == 0:
                    og_act(6)
                    og_act(7)
                bss = newps()
                for vh in range(2):
                    mm(PS(bss), ones256, osq(h, vh), start=(vh == 0), stop=(vh == 1))
                rstd = tmp(2 + (h % 2))
                actf(rstd, PS(bss), AF.Ln, bias=RMS_EPS)
                actf(rstd, rstd, AF.Exp, scale=-0.5)
                for vh in range(2):
                    c = 2 * h + vh
                    tn = tmp(4 + 2 * (h % 2) + vh)
                    stt_(tn, oT(c), col(C_ONORM + vh), rstd, ALU.mult, ALU.mult)
                    tt_(hT(c), tn, V(sg_ap[:, c, :], [("sg", c)]), ALU.mult)
                if h >= 1:
                    hp = h - 1
                    for oc in range(4):
                        for vh in range(2):
                            c = 2 * hp + vh
                            mm(PS(obank[oc]), W(so0, vo0, c, oc * 128, 128), hT(c), start=(c == 0), stop=False)
            for oc in range(4):
                for vh in range(2):
                    c = 6 + vh
                    mm(PS(obank[oc]), W(so0, vo0, c, oc * 128, 128), hT(c), start=False, stop=(c == 7))
            for oc in range(4):
                _reserved.discard(obank[oc])
                resid_add(oc, obank[oc], stats)
            wdone(so0)
            for oc in range(4, 8):
                b = proj_fm(so1, vo1, (oc - 4) * 128, hT, 8)
                resid_add(oc, b, stats)
            wdone(so1)

        order = ["l0mix", "l0ffn", "l0ple", "l1mix", "l1ffn", "l1ple"]
        nst = len(order) if stop_after is None else order.index(stop_after) + 1
        for si in range(nst):
            tile_slabs.extend(stage_slabs(order[si]))
        for t in range(NT):
            slabs.extend(tile_slabs)
        prepass()
        pos_dma(0)
        rope_tables(0)
        for c in range(4):
            x_dma(0, c)
        for c in range(8):
            load_chunk(0, c)
        for t in range(NT):
            cur["t"] = t
            fin["t"] = t
            for si in range(nst):
                nm = order[si]
                l = int(nm[1])
                stats = (si + 1 < nst)
                fin["last"] = (si == nst - 1)
                if fin["last"] and t + 1 < NT:
                    for c in range(4):
                        x_dma(t + 1, c)
                if nm.endswith("mix"):
                    p_dma(l, t)
                    if l == 0 and t + 1 < NT:
                        pos_dma(t + 1)
                    (l0_mixer if l == 0 else l1_mixer)(t, stats)
                    if l == 0 and t + 1 < NT and nst < 2:
                        rope_tables(t + 1)
                elif nm.endswith("ffn"):
                    hook = None
                    if l == 0 and t + 1 < NT:
                        hook = (lambda tt=t + 1: rope_tables(tt))
                    ffn(l, t, stats, hook)
                else:
                    ple(l, t, stats)
            finish_flush()
            fin["last"] = False
        P.op("sp", None, reads=[("y", t, c) for t in range(NT) for c in range(8)])
        P.emit(nc, st)
    return nc


def _colmajor(v):
    return np.ascontiguousarray(np.asarray(v, np.float32).reshape(-1, 128).T)


def host_tables(inp):
    f32 = np.float32
    cols = np.zeros((128, NCOLS), f32)
    for l in range(2):
        cols[:, C_MIXN + l * 8:C_MIXN + l * 8 + 8] = _colmajor(inp["mix_norm"][l])
        cols[:, C_FFNN + l * 8:C_FFNN + l * 8 + 8] = _colmajor(inp["ffn_norm"][l])
        cols[:, C_PLEN + l * 8:C_PLEN + l * 8 + 8] = _colmajor(inp["ple_norm"][l])
        for k in range(3):
            cols[:, C_CW + (l * 3 + k) * 44:C_CW + (l * 3 + k) * 44 + 44] = _colmajor(inp["ffn_conv_w"][l, k])
        cols[:, C_CB + l * 44:C_CB + l * 44 + 44] = _colmajor(inp["ffn_conv_b"][l])
    for k in range(3):
        cols[:, C_HYCW + k * 4:C_HYCW + k * 4 + 4] = _colmajor(inp["hy_conv_w"][0, k])
    cols[:, C_QN] = np.tile(np.asarray(inp["hy_q_norm"][0], f32), 2)
    cols[:, C_KN] = np.tile(np.asarray(inp["hy_k_norm"][0], f32), 2)
    inv_freq = (10000.0 ** (-np.arange(0, 64, 2, dtype=np.float32) / np.float32(64))).astype(f32)
    pidx = np.arange(128)
    cols[:, C_INVF] = inv_freq[pidx % 32]
    cols[:, C_SGN] = np.where((pidx % 64) < 32, -1.0, 1.0)
    sinks = np.asarray(inp["hy_sinks"][0], f32)
    for j in range(4):
        cols[:, C_SINK + j] = sinks[j + 4 * (pidx // 64)]
    cols[:, C_ONORM:C_ONORM + 2] = _colmajor(inp["gla_o_norm"][0])
    cb = np.zeros((128, NCB), f32)
    cb[:, B_ONESK:B_ONESK + 128] = 1.0 / 1024.0
    blk = (pidx[:, None] // 64) == (pidx[None, :] // 64)
    cb[:, B_BLK64:B_BLK64 + 128] = blk.astype(f32) / 64.0
    swap = np.where((pidx % 64) < 32, pidx + 32, pidx - 32)
    cb[:, B_PSWAP:B_PSWAP + 128] = (pidx[:, None] == swap[None, :]).astype(f32)
    cb[:, B_ONESPAD:B_ONESPAD + 64] = 1.0
    cb[:, B_ONESPAD + 128 + 64:B_ONESPAD + 256] = 1.0
    cb[:, B_ONES256:B_ONES256 + 128] = 1.0 / 256.0
    own = (pidx[:, None] <= pidx[None, :]).astype(f32)
    prev = (pidx[:, None] > pidx[None, :]).astype(f32)
    cb[:, B_MOWN:B_MOWN + 512] = np.tile(own, (1, 4))
    cb[:, B_MPREV:B_MPREV + 512] = np.tile(prev, (1, 4))
    cb[:, B_IDENT:B_IDENT + 128] = np.eye(128, dtype=f32)
    tri = own / 16.0
    ident = np.eye(128, dtype=f32)
    perm = np.concatenate([np.arange(64) + 64 * (j + 4 * half) for j in range(4) for half in range(2)])
    w_in = np.asarray(inp["hy_w_in"][0], f32)
    w_in_p = np.ascontiguousarray(np.concatenate([w_in[:, :512][:, perm], w_in[:, 512:]], axis=1))
    w_out = np.asarray(inp["hy_w_out"][0], f32)
    w_out_p = np.ascontiguousarray(np.concatenate([w_out[:512][perm], w_out[512:]], axis=0))
    wgu = np.ascontiguousarray(np.concatenate([np.asarray(inp["gla_w_gate_up"][0], f32),
                                               np.asarray(inp["gla_gate_bias"][0], f32)[None, :]], axis=0))
    shared = {
        "hy_w_in": w_in_p, "hy_w_out": w_out_p,
        "gla_w_in": np.ascontiguousarray(np.asarray(inp["gla_w_in"][0], f32)),
        "gla_w_out": np.ascontiguousarray(np.asarray(inp["gla_w_out"][0], f32)),
        "gla_wgu_aug": wgu, "cols": cols, "ident": ident, "triu": np.ascontiguousarray(tri), "cbf": cb,
    }
    for l in range(2):
        shared["ffn_w_up%d" % l] = np.ascontiguousarray(np.asarray(inp["ffn_w_up"][l], f32))
        shared["ffn_w_down%d" % l] = np.ascontiguousarray(np.asarray(inp["ffn_w_down"][l], f32))
        shared["ple_w_gate%d" % l] = np.ascontiguousarray(np.asarray(inp["ple_w_gate"][l], f32))
        shared["ple_w_proj%d" % l] = np.ascontiguousarray(np.asarray(inp["ple_w_proj"][l], f32))
    return shared


def run(inp, S, cores, stop_after=None, trace=False):
    shared = host_tables(inp)
    nc = build(S=S, stop_after=stop_after)
    in_maps = []
    for b in cores:
        m = dict(shared)
        m["x"] = np.ascontiguousarray(np.asarray(inp["x"][b, :S], np.float32))
        m["p"] = np.ascontiguousarray(np.asarray(inp["p"][:, b, :S], np.float32))
        m["pos"] = np.ascontiguousarray(np.asarray(inp["positions"][b, :S], np.int32).reshape(1, S))
        in_maps.append(m)
    res = run_bass_kernel_spmd(nc, in_maps, core_ids=list(range(len(cores))), trace=trace)
    return res


def kernel(**inputs):
    res = run(inputs, 4096, list(range(8)))
    return np.stack([r["y"] for r in res.results], axis=0).astype(np.float32)
```

```python
import numpy as np
from contextlib import ExitStack
import concourse.bass as bass
import concourse.mybir as mybir
from concourse.bass_utils import run_bass_kernel_spmd

F32 = mybir.dt.float32
BF16 = mybir.dt.bfloat16
I32 = mybir.dt.int32
AF = mybir.ActivationFunctionType
ALU = mybir.AluOpType

ENGS = ("pe", "act", "dve", "pool", "sp")
N_DMA_SEMS = 24


class _Op:
    __slots__ = ("eng", "fn", "reads", "writes", "dma", "deps", "signal", "milestone",
                 "dma_sem", "dma_val", "dma_prev", "idx", "fence", "tag")


class Prog:
    def __init__(self):
        self.ops = []
        self.tag = ""
        self.annotate = False

    def op(self, eng, fn, reads=(), writes=(), dma=False):
        o = _Op()
        o.eng = eng
        o.fn = fn
        o.reads = tuple(reads)
        o.writes = tuple(writes)
        o.dma = dma
        o.deps = None
        o.signal = False
        o.milestone = 0
        o.dma_sem = -1
        o.dma_val = 0
        o.dma_prev = 0
        o.idx = len(self.ops)
        o.fence = False
        o.tag = self.tag
        self.ops.append(o)
        return o

    def pe(self, fn, reads=(), writes=()):
        return self.op("pe", fn, reads, writes)

    def act(self, fn, reads=(), writes=()):
        return self.op("act", fn, reads, writes)

    def dve(self, fn, reads=(), writes=()):
        return self.op("dve", fn, reads, writes)

    def pool(self, fn, reads=(), writes=()):
        return self.op("pool", fn, reads, writes)

    def dma(self, eng, fn, reads=(), writes=()):
        return self.op(eng, fn, reads, writes, dma=True)

    def analyze(self):
        last_writer = {}
        readers = {}
        ops = self.ops
        for o in ops:
            deps = set()
            for r in o.reads:
                w = last_writer.get(r)
                if w is not None:
                    deps.add(w)
                if o.fence:
                    for rd in readers.get(r, ()):
                        deps.add(rd)
            for k in o.writes:
                w = last_writer.get(k)
                if w is not None:
                    deps.add(w)
                for rd in readers.get(k, ()):
                    deps.add(rd)
            deps.discard(o.idx)
            dl = []
            for d in sorted(deps):
                od = ops[d]
                if (not od.dma) and (not o.dma) and od.eng == "pe" and o.eng == "pe":
                    continue
                dl.append(d)
                if not od.dma:
                    assert od.fn is not None
                    od.signal = True
            o.deps = dl
            if o.fence:
                continue
            for r in o.reads:
                readers.setdefault(r, []).append(o.idx)
            for k in o.writes:
                last_writer[k] = o.idx
                readers[k] = []
        cnt = {e: 0 for e in ENGS}
        for o in ops:
            if o.dma:
                continue
            if o.signal:
                cnt[o.eng] += 1
                o.milestone = cnt[o.eng]
        pools = {"sp": list(range(0, 16)), "pool": list(range(16, 20)), "act": list(range(20, N_DMA_SEMS))}
        use = [0] * N_DMA_SEMS
        cnt_q = {}
        k = 0
        for o in ops:
            if o.dma:
                pl = pools[o.eng]
                s = pl[cnt_q.get(o.eng, 0) % len(pl)]
                cnt_q[o.eng] = cnt_q.get(o.eng, 0) + 1
                k += 1
                o.dma_sem = s
                o.dma_prev = use[s] * 16
                use[s] += 1
                o.dma_val = use[s] * 16
        self.n_dma = k

    def emit(self, nc, stack):
        self.analyze()
        ops = self.ops
        esem = {e: stack.enter_context(nc.semaphore("s_" + e)) for e in ENGS}
        dsem = [stack.enter_context(nc.semaphore("d%d" % i)) for i in range(N_DMA_SEMS)]
        per_eng = {e: [o for o in ops if o.eng == e] for e in ENGS}
        block = stack.enter_context(nc.Block())

        def run(engname, eng):
            waited = {}

            def wait(key, sem, val):
                if val <= 0:
                    return
                if waited.get(key, 0) >= val:
                    return
                waited[key] = val
                eng.wait_ge(sem, val)

            for o in per_eng[engname]:
                for d in o.deps:
                    od = ops[d]
                    if od.dma:
                        wait(("d", od.dma_sem), dsem[od.dma_sem], od.dma_val)
                    else:
                        wait(("e", od.eng), esem[od.eng], od.milestone)
                if o.dma:
                    wait(("d", o.dma_sem), dsem[o.dma_sem], o.dma_prev)
                if o.fn is None:
                    continue
                ins = o.fn(eng)
                if self.annotate and o.tag:
                    ins.annotate(o.tag)
                if o.dma:
                    ins.then_inc(dsem[o.dma_sem], 16)
                elif o.signal:
                    ins.then_inc(esem[engname], 1)

        @block.tensor
        def _(e):
            run("pe", e)

        @block.scalar
        def _(e):
            run("act", e)

        @block.vector
        def _(e):
            run("dve", e)

        @block.gpsimd
        def _(e):
            run("pool", e)

        @block.sync
        def _(e):
            run("sp", e)


class V:
    __slots__ = ("ap", "keys")

    def __init__(self, ap, keys):
        self.ap = ap
        self.keys = tuple(keys)


def _k(x):
    return x.keys if isinstance(x, V) else ()


def _a(x):
    return x.ap if isinstance(x, V) else x


D_MODEL = 1024
TT = 512
NQ = 4
D_FF = 2816
NFF = 22
RMS_EPS = 1e-6
TWO_PI = 6.283185307179586
CW_C1 = 6.28125
CW_C2 = TWO_PI - 6.28125

C_MIXN, C_FFNN, C_PLEN = 0, 16, 32
C_CW = 48
C_CB = 312
C_HYCW = 400
C_QN, C_KN, C_INVF, C_SGN = 412, 413, 414, 415
C_SINK = 416
C_ONORM = 420
NCOLS = 422
B_ONESK, B_BLK64, B_PSWAP, B_ONESPAD, B_ONES256, B_MOWN, B_MPREV, B_IDENT = 0, 128, 256, 384, 640, 768, 1280, 1792
NCB = 1920


def build(S=4096, stop_after=None):
    NT = S // TT
    nc = bass.Bass("TRN2", target_bir_lowering=False)

    def din(name, shape, dt=F32):
        return nc.dram_tensor(name, shape, dt, kind="ExternalInput").ap()

    x_d = din("x", [S, 1024])
    p_d = din("p", [2, S, 256])
    pos_d = din("pos", [1, S], I32)
    w_hy_in = din("hy_w_in", [1024, 2304])
    w_hy_out = din("hy_w_out", [1024, 1024])
    w_up = [din("ffn_w_up%d" % l, [1024, 5632]) for l in range(2)]
    w_down = [din("ffn_w_down%d" % l, [2816, 1024]) for l in range(2)]
    w_pg = [din("ple_w_gate%d" % l, [1024, 1024]) for l in range(2)]
    w_pp = [din("ple_w_proj%d" % l, [256, 1024]) for l in range(2)]
    w_gla_in = din("gla_w_in", [1024, 3088])
    w_gla_out = din("gla_w_out", [1024, 1024])
    wgu_d = din("gla_wgu_aug", [17, 512])
    cols_d = din("cols", [128, NCOLS])
    ident_d = din("ident", [128, 128])
    tri_d = din("triu", [128, 128])
    cbf_d = din("cbf", [128, NCB])
    y_d = nc.dram_tensor("y", [S, 1024], F32, kind="ExternalOutput").ap()

    P = Prog()
    st = ExitStack()
    with st:
        def sb(name, shape, dt):
            return st.enter_context(nc.sbuf_tensor("sb_" + name, shape, dt))

        xT_t = sb("xT", [128, 8, TT], F32)
        hT_t = sb("hT", [128, 8, TT], BF16)
        xin_t = sb("xin", [128, 2, 1024], F32)
        xout_t = sb("xout", [128, 2, 1024], F32)
        pin_ap = xout_t[:, 0, :].rearrange("p (b d) -> p b d", b=NQ)
        PIN_KEY = ("xout", 0)
        cs_t = sb("cs", [128, 2, TT], F32)
        sqt_t = sb("sqt", [128, 3, TT], BF16)
        posi_t = sb("posi", [128, TT], I32)
        pT_t = sb("pT", [128, 2, TT], BF16)
        NSLOT, SLOT = 6, 4096
        ring_t = sb("ring", [128, NSLOT, SLOT], BF16)
        actp_t = sb("actp", [128, NFF * TT // 2], F32)
        NTMP = 10
        tmp_t = sb("tmp", [128, NTMP, TT], F32)
        cols_t = sb("cols", [128, NCOLS], F32)
        ident_t = sb("ident", [128, 128], F32)
        tri_t = sb("tri", [128, 128], F32)
        cbf_t = sb("cbf", [128, NCB], BF16)
        tails_t = sb("tails", [128, 2, 44, 2], F32)
        fix_t = sb("fix", [128, 2, 44, 2], F32)
        fixtmp_t = sb("fixtmp", [128, 44], F32)
        sinke_t = sb("sinke", [128, 4], F32)
        lnscr_t = sb("lnscr", [128, 2], F32)
        glr_t = sb("glrw", [128, 8, 16], BF16)
        wgu_t = sb("wgu", [32, 512], F32)
        mix_t = sb("mixs", [128, 11776], F32)

        act_all = actp_t[:, :].bitcast(BF16).rearrange("p (c t) -> p c t", t=TT)
        oT_all = actp_t[:, 0:8 * TT].rearrange("p (c t) -> p c t", t=TT)

        def xT(c, sl=slice(None)):
            return V(xT_t[:, c, sl], [("xT", c)])

        def hT(c, sl=slice(None)):
            return V(hT_t[:, c, sl], [("hT", c)])

        def act(c, sl=slice(None)):
            return V(act_all[:, c, sl], [("act", c)])

        def sq(c):
            return act(c)

        def oT(c, sl=slice(None)):
            return V(oT_all[:, c, sl], [("act", 2 * c), ("act", 2 * c + 1)])

        def tmp(i, sl=slice(None)):
            return V(tmp_t[:, i, sl], [("tmp", i)])

        def tmpbf(i, n=TT):
            return V(tmp_t[:, i, :].bitcast(BF16)[:, 0:n], [("tmp", i)])

        def col(i, rows=slice(None)):
            return V(cols_t[rows, i:i + 1], [("cols",)])

        def cbf(c0, n, rows=slice(None)):
            return V(cbf_t[rows, c0:c0 + n], [("cbf",)])

        ident = V(ident_t[:, :], [("ident",)])
        tri = V(tri_t[:, :], [("tri",)])

        mix_bf = mix_t[:, :].bitcast(BF16)
        _off = [0]

        def carve_bf(n):
            o = _off[0]
            _off[0] += n
            return mix_bf[:, o:o + n]

        def carve_f32(n):
            assert _off[0] % 2 == 0
            o = _off[0] // 2
            _off[0] += 2 * n
            return mix_t[:, o:o + n]

        _off[0] = 0
        qTr_ap = carve_bf(NQ * 4 * 128).rearrange("p (b j t) -> p b j t", b=NQ, j=4)
        kTr_ap = carve_bf(5 * 128).rearrange("p (b t) -> p b t", b=5)
        Vp_ap = carve_bf(5 * 256).rearrange("p (b t) -> p b t", b=5)
        E_ap = carve_bf(12 * TT).rearrange("p (e t) -> p e t", e=12)
        convo_ap = carve_bf(4 * TT).rearrange("p (c t) -> p c t", c=4)
        attno_ap = carve_bf(4 * TT).rearrange("p (c t) -> p c t", c=4)
        m_ap = carve_f32(4 * 516).rearrange("p (c t) -> p c t", c=4)
        l0_end = _off[0]
        _off[0] = 0
        qtil_ap = carve_bf(4 * TT).rearrange("p (h t) -> p h t", h=4)
        ktil_ap = carve_bf(4 * TT).rearrange("p (h t) -> p h t", h=4)
        ktok_ap = carve_bf(NQ * 512).rearrange("p (b d) -> p b d", b=NQ)
        vtok_ap = carve_bf(NQ * 1024).rearrange("p (b d) -> p b d", b=NQ)
        sg_ap = carve_bf(8 * TT).rearrange("p (c t) -> p c t", c=8)
        scT_ap = carve_bf(2 * TT).rearrange("p (e t) -> p e t", e=2)
        osq_ap = carve_bf(4 * TT).rearrange("p (e t) -> p e t", e=4)
        nl_ap = carve_f32(NQ * 512).rearrange("p (b d) -> p b d", b=NQ)
        glrT_ap = carve_f32(TT)
        l1_end = _off[0]
        assert max(l0_end, l1_end) <= 23552, (l0_end, l1_end)
        S_t = sb("Sst", [128, 4, 256], F32)
        Sbf_t = sb("Sbf", [128, 2, 4, 256], BF16)
        egend_t = sb("egend", [128, 4, NQ], F32)
        kcar_t = sb("kcar", [128, 128], BF16)
        vcar_t = sb("vcar", [128, 256], BF16)
        mcar_t = sb("mcar", [128, 4, 2], F32)

        MIX = ("mix",)

        psum = [st.enter_context(nc.psum_tensor("ps%d" % i, [128, TT], F32)) for i in range(8)]
        _psi = [0]

        _reserved = set()
        _alloc_t = {}
        _alloc_n = [0]

        def newps():
            last = {i: -1 for i in range(8)}
            for o in reversed(P.ops[-400:]):
                for k in o.reads + o.writes:
                    if isinstance(k, tuple) and len(k) == 2 and k[0] == "ps" and last[k[1]] < 0:
                        last[k[1]] = o.idx
                if all(v >= 0 for v in last.values()):
                    break
            cands = [i for i in range(8) if i not in _reserved]
            b = min(cands, key=lambda i: (max(last[i], _alloc_t.get(i, -1)), (i - _psi[0]) % 8))
            _psi[0] = b + 1
            _alloc_n[0] += 1
            _alloc_t[b] = len(P.ops) + _alloc_n[0] * 1e-6
            return b

        def PS(i, sl=slice(None), rows=slice(None)):
            return V(psum[i][rows, sl], [("ps", i)])

        def mm(out, lhsT, rhs, start=True, stop=True):
            P.pe(lambda e: e.matmul(out.ap, lhsT.ap, rhs.ap, start=start, stop=stop),
                 reads=lhsT.keys + rhs.keys, writes=out.keys)

        def tr(out, in_):
            P.pe(lambda e: e.transpose(out.ap, in_.ap, ident.ap), reads=in_.keys + ident.keys, writes=out.keys)

        def actf(out, in_, func, bias=None, scale=None, eng="act"):
            kw = {}
            rk = in_.keys
            if bias is not None:
                kw["bias"] = _a(bias)
                rk = rk + _k(bias)
            if scale is not None:
                kw["scale"] = _a(scale)
                rk = rk + _k(scale)
            P.act(lambda e: e.activation(out.ap, in_.ap, func, **kw), reads=rk, writes=out.keys)

        def tt_(out, a, b, op):
            P.dve(lambda e: e.tensor_tensor(out.ap, a.ap, b.ap, op), reads=a.keys + b.keys, writes=out.keys)

        def ts_(out, a, s1, s2, op0, op1=None):
            rk = a.keys + _k(s1) + _k(s2)
            if op1 is None:
                P.dve(lambda e: e.tensor_scalar(out.ap, a.ap, _a(s1), None, op0), reads=rk, writes=out.keys)
            else:
                P.dve(lambda e: e.tensor_scalar(out.ap, a.ap, _a(s1), _a(s2), op0, op1), reads=rk, writes=out.keys)

        def stt_(out, a, s, b, op0, op1):
            P.dve(lambda e: e.scalar_tensor_tensor(out.ap, a.ap, _a(s), b.ap, op0, op1),
                  reads=a.keys + _k(s) + b.keys, writes=out.keys)

        def cp_dve(out, in_):
            P.dve(lambda e: e.tensor_copy(out.ap, in_.ap), reads=in_.keys, writes=out.keys)

        def cp_act(out, in_):
            actf(out, in_, AF.Copy)

        cur = {"t": 0}

        def pool_op(fn, reads, writes):
            if cur["t"] == 0:
                P.dve(fn, reads=reads, writes=writes)
            else:
                P.pool(fn, reads=reads, writes=writes)

        def preload_ln():
            actf(V(lnscr_t[:, 0:1], [("lnscr",)]), V(cols_t[:, C_SGN:C_SGN + 1], [("cols",)]), AF.Ln, bias=2.0)

        def memset(out, val, eng="dve"):
            P.op(eng, lambda e: e.memset(out.ap, val), writes=out.keys)

        def stage_slabs(stage):
            L = []
            l = int(stage[1])
            if stage == "l0mix":
                L.append(("kv", 8, 256, [(w_hy_in, 0, 8, 512, 256, 0)]))
                L.append(("q", 8, 512, [(w_hy_in, 0, 8, 0, 512, 0)]))
                L.append(("cb", 8, 512, [(w_hy_in, 0, 8, 768, 512, 0)]))
                L.append(("cc", 8, 512, [(w_hy_in, 0, 8, 1280, 512, 0)]))
                L.append(("cx", 8, 512, [(w_hy_in, 0, 8, 1792, 512, 0)]))
                L.append(("o0", 8, 512, [(w_hy_out, 0, 8, 0, 512, 0)]))
                L.append(("o1", 8, 512, [(w_hy_out, 0, 8, 512, 512, 0)]))
            elif stage == "l1mix":
                L.append(("k", 8, 512, [(w_gla_in, 0, 8, 512, 512, 0)]))
                L.append(("v0", 8, 512, [(w_gla_in, 0, 8, 1024, 512, 0)]))
                L.append(("v1", 8, 512, [(w_gla_in, 0, 8, 1536, 512, 0)]))
                L.append(("q", 8, 512, [(w_gla_in, 0, 8, 0, 512, 0)]))
                L.append(("og0", 8, 512, [(w_gla_in, 0, 8, 2064, 512, 0)]))
                L.append(("og1", 8, 512, [(w_gla_in, 0, 8, 2576, 512, 0)]))
                L.append(("o0", 8, 512, [(w_gla_out, 0, 8, 0, 512, 0)]))
                L.append(("o1", 8, 512, [(w_gla_out, 0, 8, 512, 512, 0)]))
            elif stage.endswith("ffn"):
                for i in range(11):
                    L.append(("up%d" % i, 8, 512, [(w_up[l], 0, 8, 256 * i, 256, 0), (w_up[l], 0, 8, D_FF + 256 * i, 256, 256)]))
                for g in range(4):
                    L.append(("dnA%d" % g, 11, 256, [(w_down[l], 0, 11, 256 * g, 256, 0)]))
                    L.append(("dnB%d" % g, 11, 256, [(w_down[l], 11, 11, 256 * g, 256, 0)]))
            else:
                L.append(("pp", 2, 1024, [(w_pp[l], 0, 2, 0, 1024, 0)]))
                L.append(("pg0", 8, 512, [(w_pg[l], 0, 8, 0, 512, 0)]))
                L.append(("pg1", 8, 512, [(w_pg[l], 0, 8, 512, 512, 0)]))
            return L

        slabs = []
        ws = {"issued": 0, "cursor": 0, "done": set()}
        tile_slabs = []
        wbf_box = [None]

        def slab_view(i):
            tag, KC, width, parts = slabs[i]
            return ring_t[:, i % NSLOT, 0:KC * width].rearrange("p (k n) -> p k n", k=KC)

        def prepass():
            wbf_box[0] = nc.dram_tensor("wbf", [len(tile_slabs), 128, SLOT], BF16, kind="Internal").ap()
            k = 0
            for j, (tag, KC, width, parts) in enumerate(tile_slabs):
                assert KC * width <= SLOT
                dview = wbf_box[0][j, :, 0:KC * width].rearrange("p (k n) -> p k n", k=KC)
                for (src, kr0, nkc, n0, n, doff) in parts:
                    srcv = src[kr0 * 128:(kr0 + nkc) * 128, n0:n0 + n].rearrange("(k p) n -> p k n", p=128)
                    dstv = dview[:, 0:nkc, doff:doff + n]
                    P.dma("pool", lambda e, d=dstv, s_=srcv: e.dma_start(out=d, in_=s_),
                          writes=[("wbf", j, doff)])
                    k += 1

        def pump():
            nts = len(tile_slabs)
            while ws["issued"] < len(slabs) and (ws["issued"] < NSLOT or (ws["issued"] - NSLOT) in ws["done"]):
                i = ws["issued"]
                tag, KC, width, parts = slabs[i]
                j = i % nts
                dst = ring_t[:, i % NSLOT, 0:KC * width]
                srcv = wbf_box[0][j, :, 0:KC * width]
                P.dma("sp", lambda e, d=dst, s_=srcv: e.dma_start(out=d, in_=s_),
                      reads=[("wbf", j, p[5]) for p in parts], writes=[("w", i % NSLOT)])
                ws["issued"] += 1

        def wnext(tag):
            i = ws["cursor"]
            ws["cursor"] += 1
            assert slabs[i][0] == tag, (slabs[i][0], tag)
            pump()
            assert i < ws["issued"], ("weight ring liveness too deep", tag)
            return i, slab_view(i)

        def wdone(i):
            ws["done"].add(i)
            pump()

        def W(slot, view, kc, c0, n):
            return V(view[:, kc, c0:c0 + n], [("w", slot % NSLOT)])

        P.dma("sp", lambda e: e.dma_start(out=cols_t[:, :], in_=cols_d), writes=[("cols",)])
        P.dma("sp", lambda e: e.dma_start(out=ident_t[:, :], in_=ident_d), writes=[("ident",)])
        P.dma("sp", lambda e: e.dma_start(out=tri_t[:, :], in_=tri_d), writes=[("tri",)])
        P.dma("pool", lambda e: e.dma_start(out=cbf_t[:, :], in_=cbf_d), writes=[("cbf",)])
        P.dma("sp", lambda e: e.dma_start(out=wgu_t[0:17, :], in_=wgu_d), writes=[("wgu",)])
        P.dma("pool", lambda e: e.dma_start(out=glr_t[:, :, :], in_=w_gla_in[:, 2048:2064].rearrange("(k p) n -> p k n", p=128)),
              writes=[("glrw",)])
        memset(V(tails_t[:, :, :, :], [("tails", 0), ("tails", 1)]), 0.0)
        memset(V(fix_t[:, :, :, :], [("fix", 0), ("fix", 1)]), 0.0)
        memset(V(S_t[:, :, :], [("S", h) for h in range(4)]), 0.0)
        memset(V(Sbf_t[:, :, :, :], [("Sbf", i, h) for i in range(2) for h in range(4)]), 0.0)
        memset(V(kcar_t[:, :], [("kcar",)]), 0.0)
        memset(V(vcar_t[:, :], [("vcar",)]), 0.0)
        memset(V(mcar_t[:, :, :], [("mcar",)]), 0.0)
        actf(V(sinke_t[:, :], [("sinke",)]), V(cols_t[:, C_SINK:C_SINK + 4], [("cols",)]), AF.Exp)

        onesK = cbf(B_ONESK, 128)
        blk64 = cbf(B_BLK64, 128)
        pswap = cbf(B_PSWAP, 128)
        ones256 = cbf(B_ONES256, 128)
        mown = cbf(B_MOWN, 512)
        mprev = cbf(B_MPREV, 512)

        _stt = {"bank": None, "n": 0}

        def stats_chunk(c):
            if _stt["bank"] is None:
                _stt["bank"] = newps()
                _reserved.add(_stt["bank"])
                _stt["n"] = 0
            k = _stt["n"]
            sqv = V(sqt_t[:, k % 3, :], [("sqt", k % 3)])
            actf(sqv, xT(c), AF.Square)
            if k >= 2:
                stats_mm(k - 2)
            _stt["n"] = k + 1

        def stats_mm(k):
            sqv = V(sqt_t[:, k % 3, :], [("sqt", k % 3)])
            mm(PS(_stt["bank"]), onesK, sqv, start=(k == 0), stop=(k == 7))

        def norm_finish(gcol0):
            assert _stt["n"] == 8
            stats_mm(6)
            stats_mm(7)
            ss = _stt["bank"]
            _reserved.discard(ss)
            _stt["bank"] = None
            _stt["n"] = 0
            lnv = tmp(0)
            rstd = tmp(1)
            actf(lnv, PS(ss), AF.Ln, bias=RMS_EPS)
            actf(rstd, lnv, AF.Exp, scale=-0.5)
            for c in range(8):
                stt_(hT(c), xT(c), col(gcol0 + c), rstd, ALU.mult, ALU.mult)

        def proj_fm(slot, view, c0, src, nk, bank=None):
            b = newps() if bank is None else bank
            for kc in range(nk):
                mm(PS(b), W(slot, view, kc, c0, 128), src(kc), start=(kc == 0), stop=(kc == nk - 1))
            return b

        def resid_add(oc, b, stats):
            tt_(xT(oc), xT(oc), PS(b), ALU.add)
            finish_chunk(oc, stats)

        xstg = xin_t[:, :, :].rearrange("p a (b f) -> p (a b) f", b=2)
        ystg = xout_t[:, 1, :].rearrange("p (a f) -> p a f", a=2)

        def x_dma(t, c):
            buf = c % 4
            P.dma("sp", lambda e: e.dma_start(
                out=xstg[:, buf, :].rearrange("p (b f) -> p b f", b=NQ),
                in_=x_d[t * TT:(t + 1) * TT, c * 128:(c + 1) * 128].rearrange("(b p) f -> p b f", p=128)),
                writes=[("xin", buf)])

        def load_chunk(t, c):
            buf = c % 4
            b = newps()
            for blk in range(NQ):
                tr(PS(b, slice(blk * 128, (blk + 1) * 128)), V(xstg[:, buf, blk * 128:(blk + 1) * 128], [("xin", buf)]))
            if c % 2 == 0:
                cp_act(xT(c), PS(b))
            else:
                cp_dve(xT(c), PS(b))
            if c + 4 < 8:
                x_dma(t, c + 4)
            stats_chunk(c)

        def store_chunk(t, c):
            buf = c % 2
            b = newps()
            for blk in range(NQ):
                tr(PS(b, slice(blk * 128, (blk + 1) * 128)), xT(c, slice(blk * 128, (blk + 1) * 128)))
            yv = V(ystg[:, buf, :], [("xout", 1, buf)])
            if c % 2 == 0:
                cp_dve(yv, PS(b))
            else:
                cp_act(yv, PS(b))
            P.dma("sp", lambda e: e.dma_start(
                out=y_d[t * TT:(t + 1) * TT, c * 128:(c + 1) * 128].rearrange("(b p) f -> p b f", p=128),
                in_=ystg[:, buf, :].rearrange("p (b f) -> p b f", b=NQ)),
                reads=[("xout", 1, buf)], writes=[("y", t, c)])

        fin = {"last": False, "t": 0}

        def finish_chunk(oc, stats):
            if not fin["last"]:
                if stats:
                    stats_chunk(oc)
                return
            if fin.get("pend") is not None:
                pc = fin["pend"]
                store_chunk(fin["t"], pc)
                if fin["t"] + 1 < NT:
                    load_chunk(fin["t"] + 1, pc)
            fin["pend"] = oc

        def finish_flush():
            if fin.get("pend") is not None:
                pc = fin["pend"]
                store_chunk(fin["t"], pc)
                if fin["t"] + 1 < NT:
                    load_chunk(fin["t"] + 1, pc)
                fin["pend"] = None

        def p_dma(l, t):
            P.dma("sp", lambda e: e.dma_start(out=pin_ap, in_=p_d[l, t * TT:(t + 1) * TT, :].rearrange("(b q) d -> q b d", q=128)),
                  writes=[PIN_KEY])

        def p_transposes():
            for pc in range(2):
                b = newps()
                for blk in range(NQ):
                    tr(PS(b, slice(blk * 128, (blk + 1) * 128)), V(pin_ap[:, blk, pc * 128:(pc + 1) * 128], [PIN_KEY]))
                cp_act(V(pT_t[:, pc, :], [("pT", pc)]), PS(b))

        def ffn(l, t, stats, mid_hook=None):
            norm_finish(C_FFNN + l * 8)
            p_transposes()
            for i in range(11):
                slot, view = wnext("up%d" % i)
                for ci in range(2):
                    c = 2 * i + ci
                    bg = proj_fm(slot, view, ci * 128, hT, 8)
                    bu = proj_fm(slot, view, 256 + ci * 128, hT, 8)
                    ys = []
                    for which, b, cidx in (("g", bg, c), ("u", bu, 22 + c)):
                        ti = 2 + (0 if which == "g" else 1) + 2 * (c % 2)
                        y = tmp(ti)
                        w0 = col(C_CW + (l * 3 + 0) * 44 + cidx)
                        w1 = col(C_CW + (l * 3 + 1) * 44 + cidx)
                        w2 = col(C_CW + (l * 3 + 2) * 44 + cidx)
                        bb = col(C_CB + l * 44 + cidx)
                        actf(y, PS(b), AF.Identity, bias=bb, scale=w2)
                        stt_(tmp(ti, slice(1, TT)), PS(b, slice(0, TT - 1)), w1, tmp(ti, slice(1, TT)), ALU.mult, ALU.add)
                        stt_(tmp(ti, slice(2, TT)), PS(b, slice(0, TT - 2)), w0, tmp(ti, slice(2, TT)), ALU.mult, ALU.add)
                        fixv = V(fix_t[:, l, cidx, :], [("fix", l)])
                        pool_op(lambda e, ti=ti, fixv=fixv: e.tensor_tensor(tmp_t[:, ti, 0:2], tmp_t[:, ti, 0:2], fixv.ap, ALU.add),
                                reads=[("tmp", ti)] + list(fixv.keys), writes=[("tmp", ti)])
                        cp_act(V(tails_t[:, l, cidx, :], [("tails", l)]), PS(b, slice(TT - 2, TT)))
                        ys.append(ti)
                    gi = 6 + (c % 2)
                    actf(tmp(gi), tmp(ys[0]), AF.Gelu)
                    pool_op(lambda e, c=c, gi=gi, yi=ys[1]: e.tensor_tensor(act_all[:, c, :], tmp_t[:, gi, :], tmp_t[:, yi, :], ALU.mult),
                            reads=[("tmp", gi), ("tmp", ys[1])], writes=[("act", c)])
                wdone(slot)
            tl = lambda i: V(tails_t[:, l, :, i], [("tails", l)])
            fx = lambda i: V(fix_t[:, l, :, i], [("fix", l)])
            cw = lambda k: V(cols_t[:, C_CW + (l * 3 + k) * 44:C_CW + (l * 3 + k) * 44 + 44], [("cols",)])
            ftmp = V(fixtmp_t[:, :], [("fixtmp",)])
            tt_(fx(1), tl(1), cw(0), ALU.mult)
            tt_(fx(0), tl(0), cw(0), ALU.mult)
            tt_(ftmp, tl(1), cw(1), ALU.mult)
            tt_(fx(0), fx(0), ftmp, ALU.add)
            preload_ln() if mid_hook is None else None
            sl = {}
            for g in range(2):
                sl[("A", g)] = wnext("dnA%d" % g)
                sl[("B", g)] = wnext("dnB%d" % g)
            acc = {(g, oi): newps() for g in range(2) for oi in range(2)}
            for b in acc.values():
                _reserved.add(b)
            for g in range(2):
                sa, va = sl[("A", g)]
                for oi in range(2):
                    for kc in range(11):
                        mm(PS(acc[(g, oi)]), W(sa, va, kc, oi * 128, 128), act(kc), start=(kc == 0), stop=False)
                wdone(sa)
            for g in range(2):
                sb_, vb = sl[("B", g)]
                for oi in range(2):
                    for kc in range(9):
                        mm(PS(acc[(g, oi)]), W(sb_, vb, kc, oi * 128, 128), act(11 + kc), start=False, stop=False)
            for g in range(2):
                sb_, vb = sl[("B", g)]
                for oi in range(2):
                    for kc in range(9, 11):
                        mm(PS(acc[(g, oi)]), W(sb_, vb, kc, oi * 128, 128), act(11 + kc), start=False, stop=(kc == 10))
                    _reserved.discard(acc[(g, oi)])
                    resid_add(g * 2 + oi, acc[(g, oi)], stats)
                wdone(sb_)
            for g in range(2, 4):
                if g == 2 and mid_hook is not None:
                    mid_hook()
                    preload_ln()
                sa, va = wnext("dnA%d" % g)
                sb_, vb = wnext("dnB%d" % g)
                b0, b1 = newps(), newps()
                for (b, oi) in ((b0, 0), (b1, 1)):
                    for kc in range(11):
                        mm(PS(b), W(sa, va, kc, oi * 128, 128), act(kc), start=(kc == 0), stop=False)
                wdone(sa)
                for (b, oi) in ((b0, 0), (b1, 1)):
                    for kc in range(11):
                        mm(PS(b), W(sb_, vb, kc, oi * 128, 128), act(11 + kc), start=False, stop=(kc == 10))
                    resid_add(g * 2 + oi, b, stats)
                wdone(sb_)

        def ple(l, t, stats):
            norm_finish(C_PLEN + l * 8)
            spp, vpp = wnext("pp")
            sg0, vg0 = wnext("pg0")
            sg1, vg1 = wnext("pg1")
            pTv = lambda pc: V(pT_t[:, pc, :], [("pT", pc)])
            pre = {}
            for oc in range(3):
                pre[oc] = proj_fm(spp, vpp, oc * 128, pTv, 2)
                _reserved.add(pre[oc])
            for oc in range(8):
                bp = pre[oc] if oc in pre else proj_fm(spp, vpp, oc * 128, pTv, 2)
                if oc < 4:
                    bg = proj_fm(sg0, vg0, oc * 128, hT, 8)
                else:
                    bg = proj_fm(sg1, vg1, (oc - 4) * 128, hT, 8)
                if oc == 3:
                    wdone(sg0)
                if oc == 7:
                    wdone(spp)
                    wdone(sg1)
                gate = tmp(2 + (oc % 2))
                actf(gate, PS(bg), AF.Sigmoid)
                if oc == 7:
                    preload_ln()
                prod = tmp(4 + (oc % 2))
                tt_(prod, gate, PS(bp), ALU.mult)
                _reserved.discard(bp)
                tt_(xT(oc), xT(oc), prod, ALU.add)
                finish_chunk(oc, stats)

        L0_KEYS = [("qTr",), ("kTr",), ("Vp",)] + [("E", i) for i in range(12)] + [("convo", c) for c in range(4)] + \
                  [("attno",)] + [("m", c) for c in range(4)]
        L1_KEYS = [("qtil", h) for h in range(4)] + [("ktil", h) for h in range(4)] + [("ktok", b) for b in range(NQ)] + \
                  [("vtok", b) for b in range(NQ)] + [("sg", c) for c in range(8)] + [("scT", i) for i in range(2)] + \
                  [("osq", i) for i in range(4)] + [("nl", b) for b in range(NQ)] + [("glrT",)]

        def fence(keys):
            for e in ("pe", "act", "dve"):
                o = P.op(e, None, reads=keys)
                o.fence = True

        sin_v = V(cs_t[:, 1, :], [("sin",)])
        cos_v = V(cs_t[:, 0, :], [("cos",)])

        def rope_tables(t):
            posi = V(posi_t[:, :], [("posi",)])
            ang, kf, r = tmp(2), tmp(3), tmp(4)
            ki = V(tmp_t[:, 5, :].bitcast(I32), [("tmp", 5)])
            cp_dve(ang, posi)
            ts_(ang, ang, col(C_INVF), None, ALU.mult)
            ts_(kf, ang, 1.0 / TWO_PI, None, ALU.mult)
            cp_dve(ki, kf)
            cp_dve(kf, ki)
            stt_(r, kf, -CW_C1, ang, ALU.mult, ALU.add)
            stt_(r, kf, -CW_C2, r, ALU.mult, ALU.add)
            rs = tmp(6)
            ts_(rs, r, 3.1415925, -3.1415925, ALU.min, ALU.max)
            actf(sin_v, rs, AF.Sin)
            ts_(sin_v, sin_v, col(C_SGN), None, ALU.mult)
            rc, g = tmp(7), tmp(8)
            ts_(rc, r, 1.5707963267948966, None, ALU.add)
            ts_(g, rc, 3.141592653589793, TWO_PI, ALU.is_gt, ALU.mult)
            tt_(rc, rc, g, ALU.subtract)
            ts_(rc, rc, 3.1415925, -3.1415925, ALU.min, ALU.max)
            actf(cos_v, rc, AF.Sin)

        def pos_dma(t):
            P.dma("sp", lambda e: e.dma_start(out=posi_t[:, :], in_=pos_d[0:1, t * TT:(t + 1) * TT].partition_broadcast(128)),
                  writes=[("posi",)])

        def l0_mixer(t, stats):
            fence(L1_KEYS)
            norm_finish(C_MIXN + 0)
            Vp = lambda blk, g: V(Vp_ap[:, blk, g * 128:(g + 1) * 128], [("Vp",)])
            memset(V(Vp_ap[:, :, :], [("Vp",)]), 0.0)
            cp_dve(V(kTr_ap[:, 0, :], [("kTr",)]), V(kcar_t[:, :], [("kcar",)]))
            cp_dve(V(Vp_ap[:, 0, :], [("Vp",)]), V(vcar_t[:, :], [("vcar",)]))
            cp_dve(V(m_ap[:, :, 0:2], [("m", c) for c in range(4)]), V(mcar_t[:, :, :], [("mcar",)]))
            skv, vkv = wnext("kv")
            sq_, vq_ = wnext("q")
            scb, vcb = wnext("cb")
            scc, vcc = wnext("cc")
            scx, vcx = wnext("cx")
            banks = {}

            def qk_A(j):
                if j == 4:
                    banks[("A", j)] = proj_fm(skv, vkv, 0, hT, 8)
                else:
                    banks[("A", j)] = proj_fm(sq_, vq_, j * 128, hT, 8)
                _reserved.add(banks[("A", j)])

            def qk_B_ew(j):
                s0 = 2 + 4 * (items.index(j) % 2)
                sqh = V(tmp_t[:, s0, :].bitcast(BF16)[:, 0:TT], [("tmp", s0)])
                actf(sqh, PS(banks[("A", j)]), AF.Square)

            def qk_B_pe(j):
                s0 = 2 + 4 * (items.index(j) % 2)
                sqh = V(tmp_t[:, s0, :].bitcast(BF16)[:, 0:TT], [("tmp", s0)])
                b2 = newps()
                mm(PS(b2), blk64, sqh)
                banks[("B", j)] = b2
                _reserved.add(b2)

            def qk_C_ew(j):
                s0 = 2 + 4 * (items.index(j) % 2)
                rstd = tmp(s0 + 1)
                actf(rstd, PS(banks[("B", j)]), AF.Ln, bias=RMS_EPS)
                actf(rstd, rstd, AF.Exp, scale=-0.5)
                qn = V(tmp_t[:, s0, :].bitcast(BF16)[:, TT:2 * TT], [("tmp", s0)])
                stt_(qn, PS(banks[("A", j)]), col(C_QN if j < 4 else C_KN), rstd, ALU.mult, ALU.mult)
                _reserved.discard(banks[("A", j)])
                _reserved.discard(banks[("B", j)])

            def qk_C_pe(j):
                s0 = 2 + 4 * (items.index(j) % 2)
                qn = V(tmp_t[:, s0, :].bitcast(BF16)[:, TT:2 * TT], [("tmp", s0)])
                b3 = newps()
                mm(PS(b3), pswap, qn)
                banks[("C", j)] = b3
                _reserved.add(b3)

            def qk_D(j):
                s0 = 2 + 4 * (items.index(j) % 2)
                qn = V(tmp_t[:, s0, :].bitcast(BF16)[:, TT:2 * TT], [("tmp", s0)])
                t1, t2 = tmp(s0 + 2), tmp(s0 + 3)
                tt_(t1, qn, cos_v, ALU.mult)
                tt_(t2, PS(banks[("C", j)]), sin_v, ALU.mult)
                if j < 4:
                    outv = V(qTr_ap[:, :, j, :], [("qTr",)])
                else:
                    outv = V(kTr_ap[:, 1:5, :], [("kTr",)])
                P.dve(lambda e: e.tensor_tensor(outv.ap, tmp_t[:, s0 + 2, :].rearrange("p (b t) -> p b t", b=NQ),
                                                tmp_t[:, s0 + 3, :].rearrange("p (b t) -> p b t", b=NQ), ALU.add),
                      reads=t1.keys + t2.keys, writes=outv.keys)
                _reserved.discard(banks[("C", j)])

            def v_proj():
                bv = newps()
                for blk in range(NQ):
                    for kc in range(8):
                        mm(PS(bv, slice(blk * 128, (blk + 1) * 128)), hT(kc, slice(blk * 128, (blk + 1) * 128)),
                           W(skv, vkv, kc, 128, 128), start=(kc == 0), stop=(kc == 7))
                pv4 = psum[bv][:, :].rearrange("p (b g d) -> p b g d", b=NQ, g=2)
                P.act(lambda e: e.activation(Vp_ap[:, 1:5, 0:64], pv4[:, :, 0, :], AF.Copy), reads=[("ps", bv)], writes=[("Vp",)])
                P.dve(lambda e: e.tensor_copy(Vp_ap[:, 1:5, 192:256], pv4[:, :, 1, :]), reads=[("ps", bv)], writes=[("Vp",)])

            cbanks = {}

            def conv_pe(c):
                cbanks[c] = (proj_fm(scc, vcc, c * 128, hT, 8), proj_fm(scx, vcx, c * 128, hT, 8),
                             proj_fm(scb, vcb, c * 128, hT, 8))
                for b in cbanks[c]:
                    _reserved.add(b)

            def conv_ew(c):
                b_cc, b_cx, b_cb = cbanks[c]
                mk = [("m", c)]
                mcur = V(m_ap[:, c, 2:2 + TT], mk)
                cp_act(mcur, PS(b_cc))
                tt_(mcur, mcur, PS(b_cx), ALU.mult)
                y = tmp(c % 2)
                ts_(y, mcur, col(C_HYCW + 2 * 4 + c), None, ALU.mult)
                stt_(y, V(m_ap[:, c, 1:1 + TT], mk), col(C_HYCW + 1 * 4 + c), y, ALU.mult, ALU.add)
                stt_(y, V(m_ap[:, c, 0:TT], mk), col(C_HYCW + 0 * 4 + c), y, ALU.mult, ALU.add)
                tt_(V(convo_ap[:, c, :], [("convo", c)]), y, PS(b_cb), ALU.mult)
                cp_act(V(mcar_t[:, c, :], [("mcar",)]), V(m_ap[:, c, TT:TT + 2], mk))
                for b in cbanks[c]:
                    _reserved.discard(b)

            items = [4, 0, 1, 2, 3]
            for step in range(8):
                if 0 <= step - 2 < 5:
                    qk_C_ew(items[step - 2])
                if 0 <= step - 1 < 5:
                    qk_B_ew(items[step - 1])
                if step < 5:
                    qk_A(items[step])
                if step == 0:
                    pass
                if step == 6:
                    v_proj()
                    wdone(skv)
                if step == 4:
                    wdone(sq_)
                if step >= 2 and step - 2 < 4:
                    conv_pe(step - 2)
                if 0 <= step - 1 < 5:
                    qk_B_pe(items[step - 1])
                if 0 <= step - 2 < 5:
                    qk_C_pe(items[step - 2])
                if 0 <= step - 3 < 5:
                    qk_D(items[step - 3])
                if step >= 2 and step - 2 < 4:
                    conv_ew(step - 2)
            wdone(scb)
            wdone(scc)
            wdone(scx)
            so0, vo0 = wnext("o0")
            so1, vo1 = wnext("o1")
            osrc = lambda kc: V(attno_ap[:, kc, :], [("attno",)]) if kc < 4 else V(convo_ap[:, kc - 4, :], [("convo", kc - 4)])
            opre = {}
            for oc in range(3):
                b = newps()
                _reserved.add(b)
                for kc in (4, 5, 6, 7):
                    mm(PS(b), W(so0, vo0, kc, oc * 128, 128), osrc(kc), start=(kc == 4), stop=False)
                opre[oc] = b
            Eb = {}

            def attn_scores(n):
                gblk = t * NQ + n
                Es = []
                for g in range(2):
                    rows = slice(64 * g, 64 * g + 64)
                    for which in (("own", n + 1, mown), ("prev", n, mprev)):
                        if which[0] == "prev" and gblk == 0:
                            continue
                        b = newps()
                        P.pe(lambda e, b=b, rows=rows, kb=which[1], n=n: e.matmul(
                            psum[b][:, :], kTr_ap[rows, kb, :], qTr_ap[rows, n, :, :].rearrange("p j t -> p (j t)"),
                            start=True, stop=True), reads=[("kTr",), ("qTr",)], writes=[("ps", b)])
                        ei = (n % 3) * 4 + len(Es)
                        Ev = V(E_ap[:, ei, :], [("E", ei)])
                        actf(Ev, PS(b), AF.Exp, scale=0.125)
                        if g == 0:
                            pool_op(lambda e, Ev=Ev, mk=which[2]: e.tensor_tensor(Ev.ap, Ev.ap, mk.ap, ALU.mult),
                                    reads=Ev.keys + which[2].keys, writes=Ev.keys)
                        else:
                            tt_(Ev, Ev, which[2], ALU.mult)
                        Es.append((Ev, g, which[1]))
                Eb[n] = Es

            def attn_finish(n):
                Es = Eb[n]
                bo = newps()
                for i, (Ev, g, kb) in enumerate(Es):
                    mm(PS(bo), Vp(kb, g), Ev, start=(i == 0), stop=(i == len(Es) - 1))
                bd = newps()
                for i, (Ev, g, kb) in enumerate(Es):
                    mm(PS(bd), cbf(B_ONESPAD + g * 128, 128), Ev, start=(i == 0), stop=(i == len(Es) - 1))
                rden = tmp(2 + (n % 2))
                for j in range(4):
                    jsl = slice(j * 128, (j + 1) * 128)
                    actf(tmp(2 + (n % 2), jsl), PS(bd, jsl), AF.Ln, bias=V(sinke_t[:, j:j + 1], [("sinke",)]))
                actf(rden, rden, AF.Exp, scale=-1.0)
                P.dve(lambda e, bo=bo, rden=rden, n=n: e.tensor_tensor(
                    attno_ap[:, :, n * 128:(n + 1) * 128], psum[bo][:, :].rearrange("p (j t) -> p j t", j=4),
                    rden.ap.rearrange("p (j t) -> p j t", j=4), ALU.mult), reads=[("ps", bo)] + list(rden.keys), writes=[("attno",)])

            for n in range(NQ + 2):
                if n < NQ:
                    attn_scores(n)
                if n - 2 >= 0:
                    attn_finish(n - 2)
            cp_act(V(kcar_t[:, :], [("kcar",)]), V(kTr_ap[:, 4, :], [("kTr",)]))
            cp_act(V(vcar_t[:, :], [("vcar",)]), V(Vp_ap[:, 4, :], [("Vp",)]))
            for oc in range(3, 7):
                b = newps()
                _reserved.add(b)
                for kc in (4, 5, 6, 7):
                    if oc < 4:
                        wv = W(so0, vo0, kc, oc * 128, 128)
                    else:
                        wv = W(so1, vo1, kc, (oc - 4) * 128, 128)
                    mm(PS(b), wv, osrc(kc), start=(kc == 4), stop=False)
                opre[oc] = b
            for oc in range(8):
                if oc in opre:
                    b = opre[oc]
                    kcs = (0, 1, 2, 3)
                else:
                    b = newps()
                    kcs = (4, 5, 6, 7, 0, 1, 2, 3)
                for kc in kcs:
                    if oc < 4:
                        wv = W(so0, vo0, kc, oc * 128, 128)
                    else:
                        wv = W(so1, vo1, kc, (oc - 4) * 128, 128)
                    mm(PS(b), wv, osrc(kc), start=(kc == 4 and oc not in opre), stop=(kc == 3))
                _reserved.discard(b)
                resid_add(oc, b, stats)
                if oc == 3:
                    wdone(so0)
            wdone(so1)

        def l1_mixer(t, stats):
            fence(L0_KEYS)
            norm_finish(C_MIXN + 8)
            qtil = lambda h, sl=slice(None): V(qtil_ap[:, h, sl], [("qtil", h)])
            ktil = lambda h, sl=slice(None): V(ktil_ap[:, h, sl], [("ktil", h)])
            memset(V(glrT_ap[0:32, :], [("glrT",)]), 1.0)
            bgl = newps()
            for kc in range(8):
                mm(PS(bgl, rows=slice(0, 16)), V(glr_t[:, kc, :], [("glrw",)]), hT(kc), start=(kc == 0), stop=(kc == 7))
            cp_dve(V(glrT_ap[0:16, :], [("glrT",)]), PS(bgl, rows=slice(0, 16)))
            wgu = V(wgu_t[0:17, :], [("wgu",)])
            sk_, vk_ = wnext("k")
            sv0, vv0 = wnext("v0")
            sv1, vv1 = wnext("v1")
            for n in range(NQ):
                bs = slice(n * 128, (n + 1) * 128)
                bx = newps()
                mm(PS(bx), V(glrT_ap[0:17, bs], [("glrT",)]), wgu)
                ex = tmp(2 + (n % 2))
                actf(ex, PS(bx), AF.Exp, scale=-1.0)
                nlv = V(nl_ap[:, n, :], [("nl", n)])
                actf(nlv, ex, AF.Ln, bias=1.0)
                for half in range(2):
                    bv = newps()
                    for kc in range(8):
                        mm(PS(bv), hT(kc, bs), W(sv0 if half == 0 else sv1, vv0 if half == 0 else vv1, kc, 0, 512),
                           start=(kc == 0), stop=(kc == 7))
                    vv = V(vtok_ap[:, n, half * 512:(half + 1) * 512], [("vtok", n)])
                    if half == 0:
                        cp_act(vv, PS(bv))
                    else:
                        cp_dve(vv, PS(bv))
            wdone(sv0)
            wdone(sv1)
            sq_, vq_ = wnext("q")
            for h in range(4):
                bq = proj_fm(sq_, vq_, h * 128, hT, 8)
                bk = proj_fm(sk_, vk_, h * 128, hT, 8)
                bat = newps()
                for n in range(NQ):
                    mm(PS(bat, slice(n * 128, (n + 1) * 128)), V(nl_ap[:, n, h * 128:(h + 1) * 128], [("nl", n)]), tri)
                eg, en = tmp(2 + (h % 2)), tmp(4 + (h % 2))
                actf(eg, PS(bat), AF.Exp, scale=-1.0)
                actf(en, PS(bat), AF.Exp)
                P.dve(lambda e, h=h, eg=eg: e.tensor_copy(egend_t[:, h, :], eg.ap.rearrange("p (b t) -> p b t", b=NQ)[:, :, 127]),
                      reads=eg.keys, writes=[("egend",)])
                stt_(qtil(h), PS(bq), 128.0 ** -0.5, eg, ALU.mult, ALU.mult)
                tt_(ktil(h), PS(bk), en, ALU.mult)
            wdone(sk_)
            wdone(sq_)
            sg0, vg0 = wnext("og0")
            sg1, vg1 = wnext("og1")

            ogb = {}

            def og_pe(c):
                if c < 4:
                    b = proj_fm(sg0, vg0, c * 128, hT, 8)
                else:
                    b = proj_fm(sg1, vg1, (c - 4) * 128, hT, 8)
                ogb[c] = b
                _reserved.add(b)
                if c == 3:
                    wdone(sg0)
                if c == 7:
                    wdone(sg1)

            def og_act(c):
                actf(V(sg_ap[:, c, :], [("sg", c)]), PS(ogb[c]), AF.Silu)
                _reserved.discard(ogb[c])
                if c == 7:
                    preload_ln()

            def og_chunk(c):
                og_pe(c)
                og_act(c)

            og_chunk(0)
            identb = cbf(B_IDENT, 128)
            for n in range(NQ):
                bt = newps()
                pbf = psum[bt][:, :].bitcast(BF16)
                for h in range(4):
                    P.pe(lambda e, pbf=pbf, h=h, n=n: e.transpose(pbf[:, h * 128:(h + 1) * 128], ktil_ap[:, h, n * 128:(n + 1) * 128], identb.ap),
                         reads=[("ktil", h)] + list(identb.keys), writes=[("ps", bt)])
                P.act(lambda e, pbf=pbf, n=n: e.activation(ktok_ap[:, n, :], pbf[:, 0:512], AF.Copy),
                      reads=[("ps", bt)], writes=[("ktok", n)])
            def sc_block(n):
                bs = slice(n * 128, (n + 1) * 128)
                bsc = newps()
                for h in range(4):
                    mm(PS(bsc, slice(h * 128, (h + 1) * 128)), ktil(h, bs), qtil(h, bs))
                tt_(V(scT_ap[:, n % 2, :], [("scT", n % 2)]), PS(bsc), mown, ALU.mult)

            def state_block(n):
                for hh in range(2):
                    bsd = newps()
                    for hi in range(2):
                        h = hh * 2 + hi
                        mm(PS(bsd, slice(hi * 256, (hi + 1) * 256)), V(ktok_ap[:, n, h * 128:(h + 1) * 128], [("ktok", n)]),
                           V(vtok_ap[:, n, h * 256:(h + 1) * 256], [("vtok", n)]))
                    for hi in range(2):
                        h = hh * 2 + hi
                        Sv = V(S_t[:, h, :], [("S", h)])
                        ecol = V(egend_t[:, h, n:n + 1], [("egend",)])
                        ts_(Sv, Sv, ecol, None, ALU.mult)
                        stt_(Sv, PS(bsd, slice(hi * 256, (hi + 1) * 256)), ecol, Sv, ALU.mult, ALU.add)
                        cp_act(V(Sbf_t[:, (n + 1) % 2, h, :], [("Sbf", (n + 1) % 2, h)]), Sv)

            def o_block(n):
                bs = slice(n * 128, (n + 1) * 128)
                for hh in range(2):
                    bo = newps()
                    for hi in range(2):
                        h = hh * 2 + hi
                        for vh in range(2):
                            osl = slice((hi * 2 + vh) * 128, (hi * 2 + vh + 1) * 128)
                            mm(PS(bo, osl), V(Sbf_t[:, n % 2, h, vh * 128:(vh + 1) * 128], [("Sbf", n % 2, h)]), qtil(h, bs),
                               start=True, stop=False)
                            mm(PS(bo, osl), V(vtok_ap[:, n, h * 256 + vh * 128:h * 256 + (vh + 1) * 128], [("vtok", n)]),
                               V(scT_ap[:, n % 2, h * 128:(h + 1) * 128], [("scT", n % 2)]), start=False, stop=True)
                    outv = V(oT_all[:, hh * 4:hh * 4 + 4, bs], [("act", i) for i in range(hh * 8, hh * 8 + 8)])
                    inv = V(psum[bo][:, :].rearrange("p (c t) -> p c t", c=4), [("ps", bo)])
                    if hh == 0:
                        cp_act(outv, inv)
                    else:
                        cp_dve(outv, inv)

            sc_block(0)
            for n in range(NQ):
                state_block(n)
                if 1 <= n < NQ - 1:
                    og_chunk(2 * n)
                if n + 1 < NQ:
                    sc_block(n + 1)
                if n < NQ - 1:
                    og_chunk(2 * n + 1)
                o_block(n)
            og_pe(6)
            og_pe(7)
            so0, vo0 = wnext("o0")
            so1, vo1 = wnext("o1")
            obank = [newps() for _ in range(4)]
            for b in obank:
                _reserved.add(b)
            def osq(h, vh):
                i = (h % 2) * 2 + vh
                return V(osq_ap[:, i, :], [("osq", i)])

            for vh in range(2):
                actf(osq(0, vh), oT(vh), AF.Square)
            for h in range(4):
                if h + 1 < 4:
                    for vh in range(2):
                        actf(osq(h + 1, vh), oT(2 * (h + 1) + vh), AF.Square)
                if h == 0:
                    og_act(6)
                    og_act(7)
                bss = newps()
                for vh in range(2):
                    mm(PS(bss), ones256, osq(h, vh), start=(vh == 0), stop=(vh == 1))
                rstd = tmp(2 + (h % 2))
                actf(rstd, PS(bss), AF.Ln, bias=RMS_EPS)
                actf(rstd, rstd, AF.Exp, scale=-0.5)
                for vh in range(2):
                    c = 2 * h + vh
                    tn = tmp(4 + 2 * (h % 2) + vh)
                    stt_(tn, oT(c), col(C_ONORM + vh), rstd, ALU.mult, ALU.mult)
                    tt_(hT(c), tn, V(sg_ap[:, c, :], [("sg", c)]), ALU.mult)
                if h >= 1:
                    hp = h - 1
                    for oc in range(4):
                        for vh in range(2):
                            c = 2 * hp + vh
                            mm(PS(obank[oc]), W(so0, vo0, c, oc * 128, 128), hT(c), start=(c == 0), stop=False)
            for oc in range(4):
                for vh in range(2):
                    c = 6 + vh
                    mm(PS(obank[oc]), W(so0, vo0, c, oc * 128, 128), hT(c), start=False, stop=(c == 7))
            for oc in range(4):
                _reserved.discard(obank[oc])
                resid_add(oc, obank[oc], stats)
            wdone(so0)
            for oc in range(4, 8):
                b = proj_fm(so1, vo1, (oc - 4) * 128, hT, 8)
                resid_add(oc, b, stats)
            wdone(so1)

        order = ["l0mix", "l0ffn", "l0ple", "l1mix", "l1ffn", "l1ple"]
        nst = len(order) if stop_after is None else order.index(stop_after) + 1
        for si in range(nst):
            tile_slabs.extend(stage_slabs(order[si]))
        for t in range(NT):
            slabs.extend(tile_slabs)
        prepass()
        pos_dma(0)
        rope_tables(0)
        for c in range(4):
            x_dma(0, c)
        for c in range(8):
            load_chunk(0, c)
        for t in range(NT):
            cur["t"] = t
            fin["t"] = t
            for si in range(nst):
                nm = order[si]
                l = int(nm[1])
                stats = (si + 1 < nst)
                fin["last"] = (si == nst - 1)
                if fin["last"] and t + 1 < NT:
                    for c in range(4):
                        x_dma(t + 1, c)
                if nm.endswith("mix"):
                    p_dma(l, t)
                    if l == 0 and t + 1 < NT:
                        pos_dma(t + 1)
                    (l0_mixer if l == 0 else l1_mixer)(t, stats)
                    if l == 0 and t + 1 < NT and nst < 2:
                        rope_tables(t + 1)
                elif nm.endswith("ffn"):
                    hook = None
                    if l == 0 and t + 1 < NT:
                        hook = (lambda tt=t + 1: rope_tables(tt))
                    ffn(l, t, stats, hook)
                else:
                    ple(l, t, stats)
            finish_flush()
            fin["last"] = False
        P.op("sp", None, reads=[("y", t, c) for t in range(NT) for c in range(8)])
        P.emit(nc, st)
    return nc


def _colmajor(v):
    return np.ascontiguousarray(np.asarray(v, np.float32).reshape(-1, 128).T)


def host_tables(inp):
    f32 = np.float32
    cols = np.zeros((128, NCOLS), f32)
    for l in range(2):
        cols[:, C_MIXN + l * 8:C_MIXN + l * 8 + 8] = _colmajor(inp["mix_norm"][l])
        cols[:, C_FFNN + l * 8:C_FFNN + l * 8 + 8] = _colmajor(inp["ffn_norm"][l])
        cols[:, C_PLEN + l * 8:C_PLEN + l * 8 + 8] = _colmajor(inp["ple_norm"][l])
        for k in range(3):
            cols[:, C_CW + (l * 3 + k) * 44:C_CW + (l * 3 + k) * 44 + 44] = _colmajor(inp["ffn_conv_w"][l, k])
        cols[:, C_CB + l * 44:C_CB + l * 44 + 44] = _colmajor(inp["ffn_conv_b"][l])
    for k in range(3):
        cols[:, C_HYCW + k * 4:C_HYCW + k * 4 + 4] = _colmajor(inp["hy_conv_w"][0, k])
    cols[:, C_QN] = np.tile(np.asarray(inp["hy_q_norm"][0], f32), 2)
    cols[:, C_KN] = np.tile(np.asarray(inp["hy_k_norm"][0], f32), 2)
    inv_freq = (10000.0 ** (-np.arange(0, 64, 2, dtype=np.float32) / np.float32(64))).astype(f32)
    pidx = np.arange(128)
    cols[:, C_INVF] = inv_freq[pidx % 32]
    cols[:, C_SGN] = np.where((pidx % 64) < 32, -1.0, 1.0)
    sinks = np.asarray(inp["hy_sinks"][0], f32)
    for j in range(4):
        cols[:, C_SINK + j] = sinks[j + 4 * (pidx // 64)]
    cols[:, C_ONORM:C_ONORM + 2] = _colmajor(inp["gla_o_norm"][0])
    cb = np.zeros((128, NCB), f32)
    cb[:, B_ONESK:B_ONESK + 128] = 1.0 / 1024.0
    blk = (pidx[:, None] // 64) == (pidx[None, :] // 64)
    cb[:, B_BLK64:B_BLK64 + 128] = blk.astype(f32) / 64.0
    swap = np.where((pidx % 64) < 32, pidx + 32, pidx - 32)
    cb[:, B_PSWAP:B_PSWAP + 128] = (pidx[:, None] == swap[None, :]).astype(f32)
    cb[:, B_ONESPAD:B_ONESPAD + 64] = 1.0
    cb[:, B_ONESPAD + 128 + 64:B_ONESPAD + 256] = 1.0
    cb[:, B_ONES256:B_ONES256 + 128] = 1.0 / 256.0
    own = (pidx[:, None] <= pidx[None, :]).astype(f32)
    prev = (pidx[:, None] > pidx[None, :]).astype(f32)
    cb[:, B_MOWN:B_MOWN + 512] = np.tile(own, (1, 4))
    cb[:, B_MPREV:B_MPREV + 512] = np.tile(prev, (1, 4))
    cb[:, B_IDENT:B_IDENT + 128] = np.eye(128, dtype=f32)
    tri = own / 16.0
    ident = np.eye(128, dtype=f32)
    perm = np.concatenate([np.arange(64) + 64 * (j + 4 * half) for j in range(4) for half in range(2)])
    w_in = np.asarray(inp["hy_w_in"][0], f32)
    w_in_p = np.ascontiguousarray(np.concatenate([w_in[:, :512][:, perm], w_in[:, 512:]], axis=1))
    w_out = np.asarray(inp["hy_w_out"][0], f32)
    w_out_p = np.ascontiguousarray(np.concatenate([w_out[:512][perm], w_out[512:]], axis=0))
    wgu = np.ascontiguousarray(np.concatenate([np.asarray(inp["gla_w_gate_up"][0], f32),
                                               np.asarray(inp["gla_gate_bias"][0], f32)[None, :]], axis=0))
    shared = {
        "hy_w_in": w_in_p, "hy_w_out": w_out_p,
        "gla_w_in": np.ascontiguousarray(np.asarray(inp["gla_w_in"][0], f32)),
        "gla_w_out": np.ascontiguousarray(np.asarray(inp["gla_w_out"][0], f32)),
        "gla_wgu_aug": wgu, "cols": cols, "ident": ident, "triu": np.ascontiguousarray(tri), "cbf": cb,
    }
    for l in range(2):
        shared["ffn_w_up%d" % l] = np.ascontiguousarray(np.asarray(inp["ffn_w_up"][l], f32))
        shared["ffn_w_down%d" % l] = np.ascontiguousarray(np.asarray(inp["ffn_w_down"][l], f32))
        shared["ple_w_gate%d" % l] = np.ascontiguousarray(np.asarray(inp["ple_w_gate"][l], f32))
        shared["ple_w_proj%d" % l] = np.ascontiguousarray(np.asarray(inp["ple_w_proj"][l], f32))
    return shared


def run(inp, S, cores, stop_after=None, trace=False):
    shared = host_tables(inp)
    nc = build(S=S, stop_after=stop_after)
    in_maps = []
    for b in cores:
        m = dict(shared)
        m["x"] = np.ascontiguousarray(np.asarray(inp["x"][b, :S], np.float32))
        m["p"] = np.ascontiguousarray(np.asarray(inp["p"][:, b, :S], np.float32))
        m["pos"] = np.ascontiguousarray(np.asarray(inp["positions"][b, :S], np.int32).reshape(1, S))
        in_maps.append(m)
    res = run_bass_kernel_spmd(nc, in_maps, core_ids=list(range(len(cores))), trace=trace)
    return res


def kernel(**inputs):
    res = run(inputs, 4096, list(range(8)))
    return np.stack([r["y"] for r in res.results], axis=0).astype(np.float32)
```
